# Optimizing a Trainium2 kernel written in Bass

```python
import jax, jax.numpy as jnp
from jax import lax
import numpy as np

D_MODEL = 1024
BATCH = 4
SEQ = 4096
DEPTH = 2

HEAD_DIM = 64
N_HEADS = D_MODEL // HEAD_DIM
N_HEADS_A = N_HEADS // 2
N_HEADS_B = N_HEADS - N_HEADS_A
WIDTH_A = N_HEADS_A * HEAD_DIM
WIDTH_B = N_HEADS_B * HEAD_DIM
CHUNK = 128
Q_BLOCK = 128
CONV_WIDTH = 3
CONV_DIM = D_MODEL
D_FF = ((8 * D_MODEL // 3 + 127) // 128) * 128
MACARON_WEIGHT = 0.5
N_SUB = 3
N_EVEN = (DEPTH + 1) // 2
N_ODD = DEPTH // 2
EPS = 1e-6

kernel_name = "hybrid_gmlp_stickbreak_shortconv_macaron_adaln"


def rms_norm(x, g):
    xf = x.astype(jnp.float32)
    y = xf * lax.rsqrt(jnp.mean(xf * xf, axis=-1, keepdims=True) + EPS)
    return (y * g.astype(jnp.float32)).astype(x.dtype)


def modulate(x, g, shift, scale):
    return rms_norm(x, g) * (1 + scale[:, None, :]) + shift[:, None, :]


def swiglu(h, w_gate, w_up, w_down):
    return (jax.nn.silu(h @ w_gate) * (h @ w_up)) @ w_down


def stick_breaking_attention(q, k, v):
    S = q.shape[1]
    scale = HEAD_DIM ** -0.5
    outs = []
    for blk in range(S // Q_BLOCK):
        q0 = blk * Q_BLOCK
        kv_len = q0 + Q_BLOCK
        qb = q[:, q0:kv_len].astype(jnp.float32)
        kb = k[:, :kv_len].astype(jnp.float32)
        z = jnp.einsum('bqhd,bkhd->bhqk', qb, kb) * scale
        t_pos = q0 + jnp.arange(Q_BLOCK)[:, None]
        s_pos = jnp.arange(kv_len)[None, :]
        before = s_pos < t_pos
        log_1mb = jnp.where(before, -jax.nn.softplus(z), 0.0)
        tail = lax.cumsum(log_1mb, axis=3, reverse=True) - log_1mb
        w = jnp.where(before, jnp.exp(jax.nn.log_sigmoid(z) + tail), 0.0)
        outs.append(jnp.einsum('bhqk,bkhd->bqhd', w.astype(v.dtype), v[:, :kv_len]))
    return jnp.concatenate(outs, axis=1)


def gmlp_stickbreak_mixer(h, w_in, vnorm_g, w_s, b_s, w_out):
    Bsz, S, _ = h.shape
    proj = h @ w_in
    uv_a, qkv_b = proj[..., :2 * WIDTH_A], proj[..., 2 * WIDTH_A:]
    u, v = jnp.split(jax.nn.gelu(uv_a, approximate=False), 2, axis=-1)
    u = u.reshape(Bsz, S, N_HEADS_A, HEAD_DIM)
    v = rms_norm(v.reshape(Bsz, S, N_HEADS_A, HEAD_DIM), vnorm_g.reshape(N_HEADS_A, HEAD_DIM))
    v = v.reshape(Bsz, S // CHUNK, CHUNK, N_HEADS_A, HEAD_DIM)
    w_causal = jnp.tril(w_s)
    sv = jnp.einsum('hts,bnshd->bnthd', w_causal, v) + b_s.T[:, :, None]
    y_a = u * sv.reshape(Bsz, S, N_HEADS_A, HEAD_DIM)
    q, k, vb = jnp.split(qkv_b, 3, axis=-1)
    q = q.reshape(Bsz, S, N_HEADS_B, HEAD_DIM)
    k = k.reshape(Bsz, S, N_HEADS_B, HEAD_DIM)
    vb = vb.reshape(Bsz, S, N_HEADS_B, HEAD_DIM)
    y_b = stick_breaking_attention(q, k, vb)
    y = jnp.concatenate([y_a.reshape(Bsz, S, WIDTH_A), y_b.reshape(Bsz, S, WIDTH_B)], axis=-1)
    return y @ w_out


def short_conv_mixer(h, w_in, conv_w, w_out):
    b_gate, c_gate, xs = jnp.split(h @ w_in, 3, axis=-1)
    y = lax.conv_general_dilated(
        c_gate * xs, conv_w[:, None, :].astype(xs.dtype),
        window_strides=(1,), padding=((CONV_WIDTH - 1, 0),),
        dimension_numbers=('NWC', 'WIO', 'NWC'), feature_group_count=CONV_DIM)
    return (b_gate * y) @ w_out


def setup_inputs(seed: int = 0) -> dict:
    key = jax.random.key(seed)
    ks = jax.random.split(key, 18)
    f32 = jnp.float32
    D = D_MODEL

    def nrm(k, shape, fan_in):
        return jax.random.normal(k, shape, f32) * (fan_in ** -0.5)

    return {
        "x": jax.random.normal(ks[0], (BATCH, SEQ, D), f32),
        "c": jax.random.normal(ks[1], (BATCH, D), f32),
        "mod_w": nrm(ks[2], (DEPTH, D, 3 * N_SUB * D), D) * 0.5,
        "mod_b": 0.02 * jax.random.normal(ks[3], (DEPTH, 3 * N_SUB * D), f32),
        "norm_g": 1.0 + 0.02 * jax.random.normal(ks[4], (DEPTH, N_SUB, D), f32),
        "ffn_w_gate": nrm(ks[5], (DEPTH, 2, D, D_FF), D),
        "ffn_w_up": nrm(ks[6], (DEPTH, 2, D, D_FF), D),
        "ffn_w_down": nrm(ks[7], (DEPTH, 2, D_FF, D), D_FF),
        "hy_w_in": nrm(ks[8], (N_EVEN, D, 2 * WIDTH_A + 3 * WIDTH_B), D),
        "hy_w_out": nrm(ks[9], (N_EVEN, WIDTH_A + WIDTH_B, D), WIDTH_A + WIDTH_B),
        "gm_vnorm_g": 1.0 + 0.02 * jax.random.normal(ks[10], (N_EVEN, WIDTH_A), f32),
        "gm_w_s": nrm(ks[11], (N_EVEN, N_HEADS_A, CHUNK, CHUNK), CHUNK),
        "gm_b_s": 1.0 + 0.02 * jax.random.normal(ks[12], (N_EVEN, N_HEADS_A, CHUNK), f32),
        "sc_w_in": nrm(ks[13], (N_ODD, D, 3 * CONV_DIM), D),
        "sc_conv_w": nrm(ks[14], (N_ODD, CONV_WIDTH, CONV_DIM), CONV_WIDTH),
        "sc_w_out": nrm(ks[15], (N_ODD, CONV_DIM, D), CONV_DIM),
        "final_norm_g": 1.0 + 0.02 * jax.random.normal(ks[16], (D,), f32),
    }


def reference(x, c, mod_w, mod_b, norm_g, ffn_w_gate, ffn_w_up, ffn_w_down,
              hy_w_in, hy_w_out, gm_vnorm_g, gm_w_s, gm_b_s,
              sc_w_in, sc_conv_w, sc_w_out, final_norm_g):
    cond = jax.nn.silu(c)
    for layer in range(DEPTH):
        mod = cond @ mod_w[layer] + mod_b[layer]
        sh1, sc1, g1, sh2, sc2, g2, sh3, sc3, g3 = jnp.split(mod, 3 * N_SUB, axis=-1)
        h = modulate(x, norm_g[layer, 0], sh1, sc1)
        x = x + MACARON_WEIGHT * g1[:, None, :] * swiglu(
            h, ffn_w_gate[layer, 0], ffn_w_up[layer, 0], ffn_w_down[layer, 0])
        h = modulate(x, norm_g[layer, 1], sh2, sc2)
        i = layer // 2
        if layer % 2 == 0:
            m = gmlp_stickbreak_mixer(h, hy_w_in[i], gm_vnorm_g[i], gm_w_s[i], gm_b_s[i], hy_w_out[i])
        else:
            m = short_conv_mixer(h, sc_w_in[i], sc_conv_w[i], sc_w_out[i])
        x = x + g2[:, None, :] * m
        h = modulate(x, norm_g[layer, 2], sh3, sc3)
        x = x + MACARON_WEIGHT * g3[:, None, :] * swiglu(
            h, ffn_w_gate[layer, 1], ffn_w_up[layer, 1], ffn_w_down[layer, 1])
    return rms_norm(x, final_norm_g)
```

```python
import numpy as np
import concourse.bass as bass
import concourse.mybir as mybir
from concourse.bass_utils import run_bass_kernel_spmd

F32 = mybir.dt.float32
BF16 = mybir.dt.bfloat16
AF = mybir.ActivationFunctionType
ALU = mybir.AluOpType
AX = mybir.AxisListType

NT = 2048
D = 1024
DFF = 2816
GROUPS = [(0, 6), (6, 6), (12, 5), (17, 5)]
EPS = 1e-6
PAIRS = [[0, 1], [2, 3], [4, 5], [6, 7]]

V_MODB = 0
V_NG = 144
V_FG = 192
V_CW = 200
V_A = 224
V_EPS = 225
V_ONE = 226
V_C = 227
V_OH = 259
NV = 263
ALL8 = [[0, 1, 2, 3, 4, 5, 6, 7]]

STOP_AFTER = None


class Prog:
    COMPUTE = ("act", "dve", "pool", "pe")

    def __init__(self):
        self.ops = []
        self.tiles = {}
        self.last = {}
        self.bar = None
        self.bar_seen = set()
        self.async_since_bar = []

    def add(self, eng, fn, kw=None, reads=(), writes=(), kind="c"):
        idx = len(self.ops)
        deps = set()
        for k in reads:
            t = self.tiles.get(k)
            if t is not None and t["w"] is not None:
                deps.add(t["w"])
        for k in writes:
            t = self.tiles.get(k)
            if t is not None:
                if t["w"] is not None:
                    deps.add(t["w"])
                deps.update(t["r"].values())
                deps.update(t["ra"])
        if self.bar is not None and eng not in self.bar_seen:
            deps.update(self.bar)
            self.bar_seen.add(eng)
        for k in reads:
            t = self.tiles.setdefault(k, {"w": None, "r": {}, "ra": []})
            if kind == "c":
                t["r"][eng] = idx
            else:
                t["ra"].append(idx)
        for k in writes:
            self.tiles[k] = {"w": idx, "r": {}, "ra": []}
        red = {}
        adeps = []
        for d in deps:
            o = self.ops[d]
            if o["kind"] == "c":
                if o["eng"] == "pe" and eng == "pe" and kind == "c":
                    continue
                if o["eng"] not in red or red[o["eng"]] < d:
                    red[o["eng"]] = d
            else:
                adeps.append(d)
        self.ops.append(dict(eng=eng, fn=fn, kw=kw, deps=list(red.values()) + adeps, kind=kind))
        if kind == "c":
            self.last[eng] = idx
        else:
            self.async_since_bar.append(idx)
        return idx

    def barrier(self):
        self.bar = set(self.last.values()) | set(self.async_since_bar)
        self.bar_seen = set()
        self.async_since_bar = []

    def prepare(self, n_async):
        ops = self.ops
        needed = [False] * len(ops)
        for o in ops:
            for d in o["deps"]:
                needed[d] = True
        cnt = {e: 0 for e in self.COMPUTE}
        ause = [0] * n_async
        aprev = [None] * n_async
        k = 0
        for i, o in enumerate(ops):
            if o["kind"] == "c":
                if needed[i]:
                    cnt[o["eng"]] += 1
                    o["ms"] = cnt[o["eng"]]
                else:
                    o["ms"] = None
            else:
                j = k % n_async
                k += 1
                inc = 16 if o["kind"] == "dma" else 1
                o["prev"] = aprev[j]
                ause[j] += inc
                o["sem"] = j
                o["val"] = ause[j]
                o["inc"] = inc
                aprev[j] = i

    def emit_engine(self, sems, ename, eng):
        ops = self.ops
        waited = {}

        def wait(sem_key, sem, val):
            if waited.get(sem_key, 0) >= val:
                return
            waited[sem_key] = val
            eng.wait_ge(sem, val)

        for i, o in enumerate(ops):
            if o["eng"] != ename:
                continue
            for d in o["deps"]:
                od = ops[d]
                if od["kind"] == "c":
                    wait(od["eng"], sems[od["eng"]], od["ms"])
                else:
                    wait(("a", od["sem"]), sems["async"][od["sem"]], od["val"])
            if o["kind"] != "c" and o["prev"] is not None:
                op = ops[o["prev"]]
                wait(("a", op["sem"]), sems["async"][op["sem"]], op["val"])
            if o["fn"] is None:
                continue
            ins = getattr(eng, o["fn"])(**o["kw"])
            if o["kind"] == "c":
                if o["ms"] is not None:
                    ins.then_inc(sems[ename], 1)
            else:
                ins.then_inc(sems["async"][o["sem"]], o["inc"])


def build_nc(stop_after=None):
    nc = bass.Bass("TRN2", target_bir_lowering=False)
    dt = nc.dram_tensor
    xT = dt("xT", [D, NT], F32, kind="ExternalInput").ap()
    vec = dt("vec", [128, NV], F32, kind="ExternalInput").ap()
    constb = dt("constb", [128, 768], F32, kind="ExternalInput").ap()
    gvn_d = dt("gvn", [128, 512], F32, kind="ExternalInput").ap()
    bias_d = dt("gbias", [128, 512], F32, kind="ExternalInput").ap()
    wsT_d = dt("wsT", [128, 1024], F32, kind="ExternalInput").ap()
    modw = dt("modw", [2, D, 4608], F32, kind="ExternalInput").ap()
    ccmin = [dt(f"ccmin{l}", [128, 36], F32, kind="Internal").ap() for l in range(2)]
    ccmout = [dt(f"ccmout{l}", [256, 36], F32, kind="Internal").ap() for l in range(2)]
    wg = dt("wg", [2, 2, D, DFF], F32, kind="ExternalInput").ap()
    wu = dt("wu", [2, 2, D, DFF], F32, kind="ExternalInput").ap()
    wd = dt("wd", [2, 2, DFF, D], F32, kind="ExternalInput").ap()
    hy_in = dt("hy_in", [D, 2560], F32, kind="ExternalInput").ap()
    hy_out = dt("hy_out", [D, D], F32, kind="ExternalInput").ap()
    sc_in = dt("sc_in", [D, 3072], F32, kind="ExternalInput").ap()
    sc_out = dt("sc_out", [D, D], F32, kind="ExternalInput").ap()
    outT = dt("outT", [D, NT], F32, kind="ExternalOutput").ap()
    ccink = dt("ccink", [512, 2048], BF16, kind="Internal").ap()
    ccoutk = dt("ccoutk", [1024, 2048], BF16, kind="Internal").ap()
    ccinv = dt("ccinv", [512, 2048], BF16, kind="Internal").ap()
    ccoutv = dt("ccoutv", [1024, 2048], BF16, kind="Internal").ap()
    cc2in = dt("cc2in", [128, 16], F32, kind="Internal").ap()
    cc2out = dt("cc2out", [256, 16], F32, kind="Internal").ap()

    TOTAL = 206 * 1024
    beg, end = nc.bump_sbuf(TOTAL)
    at = nc.alloc_sbuf_tensor_at
    O_XT, O_HB, O_QB, O_RING, O_SMALL = 0, 65536, 98304, 114688, 188416

    def T(name, shape, dtype, off):
        return at(name, shape, dtype, offset=beg + off)

    XT = T("XT", [128, 8, NT], F32, O_XT)
    HB = T("HB", [128, 8, NT], BF16, O_HB)
    VALL = T("VALL", [128, 32, 512], BF16, O_HB)
    QT = T("QT", [128, 4, NT], BF16, O_QB)
    SQB = [T(f"SQB{i}", [128, 8, 512], F32, O_QB) for i in range(1)][0]
    SQB16 = T("SQB16", [128, 8, 1024], BF16, O_QB)
    WG = [T(f"WG{b}", [128, 8, 768], BF16, O_RING + b * 36864) for b in range(2)]
    WU = [T(f"WU{b}", [128, 8, 768], BF16, O_RING + b * 36864 + 12288) for b in range(2)]
    WD = [T(f"WD{b}", [128, 6, 1024], BF16, O_RING + b * 36864 + 24576) for b in range(2)]
    MWS = [T(f"MWS{l}", [128, 8, 1152], BF16, O_RING + 36864 + l * 18432) for l in range(2)]
    WIN = T("WIN", [128, 8, 2560], BF16, O_RING)
    KTS = [T(f"KTS{b}", [128, 4096], BF16, O_RING + b * 8192) for b in range(2)]
    WO = T("WO", [128, 8, 1024], BF16, O_RING)
    YB = T("YB", [128, 4, NT], BF16, O_RING + 16384)
    WK8 = T("WK8", [128, 4, 512], F32, O_RING + 32768)
    WK8b = T("WK8b", [128, 4, 1024], BF16, O_RING + 32768)
    YA = T("YA", [128, 4, NT], BF16, O_RING + 40960)
    RS = T("RS", [128, 8, 512], F32, O_RING + 57344)
    RS16 = T("RS16", [128, 8, 1024], BF16, O_RING + 57344)
    WIN1 = T("WIN1", [128, 8, 3072], BF16, O_RING)
    WO1 = T("WO1", [128, 8, 1024], BF16, O_RING + 49152)
    YC = T("YC", [128, 8, 512], BF16, O_RING + 65536)
    o = O_SMALL
    sm = {}

    def SM(name, shape, dtype, nbytes):
        nonlocal o
        t = T(name, shape, dtype, o)
        o += (nbytes + 31) // 32 * 32
        return t

    CONSTB = SM("CONSTB", [128, 768], BF16, 1536)
    VEC = SM("VEC", [128, NV], F32, NV * 4)
    MOD = SM("MOD", [128, 2, 72], F32, 576)
    AV = SM("AV", [128, 48], F32, 192)
    GT = SM("GT", [128, 48], F32, 192)
    CONDB = SM("CONDB", [128, 32], BF16, 64)
    CONDF = SM("CONDF", [128, 32], F32, 128)
    RAW = [SM(f"RAW{l}", [128, 8, 36], F32, 1152) for l in range(2)]
    MODP = SM("MODP", [128, 72], F32, 288)
    BIAS = SM("BIAS", [128, 512], F32, 2048)
    GVN = SM("GVN", [128, 512], F32, 2048)
    WCT = SM("WCT", [128, 1024], BF16, 2048)
    MSV = SM("MSV", [128, 4, 8], F32, 128)
    RSV = SM("RSV", [128, 4, 8], F32, 128)
    RV = SM("RV", [128, 4, 8], F32, 128)
    TAIL = SM("TAIL", [128, 32], F32, 128)
    HALOP = SM("HALOP", [128, 16], F32, 64)
    PREV = SM("PREV", [128, 16], F32, 64)
    ACTT = SM("ACTT", [128, 6, 512], BF16, 6144)
    T1 = SM("T1", [128, 2, 128], F32, 1024)
    assert o <= TOTAL, o

    ONES = CONSTB[:, 0:128]
    TRI = CONSTB[:, 128:256]
    MASKT = CONSTB[:, 256:384]
    MASKLE = CONSTB[:, 384:512]
    ZEROS = CONSTB[:, 512:640]

    def vcol(c, n=1):
        return VEC[:, c:c + n]

    PSA = nc.alloc_psum_tensor("psa", [128, 4096], F32)
    PS = [PSA[:, i * 512:(i + 1) * 512] for i in range(8)]

    def P2(i):
        return PSA[:, i * 512:(i + 2) * 512].rearrange("p (h n) -> p h n", h=2)

    P = Prog()
    A = P.add

    A("sp", "dma_start", dict(out=VEC[:], in_=vec), writes=["VEC"], kind="dma")
    A("pool", "dma_start", dict(out=CONSTB[:], in_=constb), writes=["CONST"], kind="dma")
    for kc in range(8):
        q = "sp" if kc % 2 == 0 else "act"
        A(q, "dma_start", dict(out=XT[:, kc, :], in_=xT[kc * 128:(kc + 1) * 128, :]),
          writes=[("XT", kc, t) for t in range(4)], kind="dma")
    A("sp", "dma_start", dict(out=BIAS[:], in_=bias_d), writes=["BIAS"], kind="dma")
    A("sp", "dma_start", dict(out=GVN[:], in_=gvn_d), writes=["GVN"], kind="dma")
    A("pool", "dma_start", dict(out=WCT[:], in_=wsT_d), writes=["WCT"], kind="dma")
    for h in range(8):
        A("pool", "tensor_tensor", dict(out=WCT[:, h * 128:(h + 1) * 128], in0=WCT[:, h * 128:(h + 1) * 128],
                                                 in1=MASKLE, op=ALU.mult), reads=["WCT", "CONST"], writes=["WCT"])
    A("act", "activation", dict(out=CONDF[:, 0:8], in_=vcol(V_C, 8), func=AF.Silu), reads=["VEC"], writes=["CONDF"])
    A("dve", "tensor_copy", dict(out=CONDB[:, 0:8], in_=CONDF[:, 0:8]), reads=["CONDF"], writes=["CONDB"])

    def mod_compute(l):
        for piece in range(4):
            buf = (l * 4 + piece) % 2
            A("pool", "dma_start", dict(out=MWS[buf][:], in_=modw[l, :, piece * 1152:(piece + 1) * 1152].rearrange("(kc p) n -> p kc n", p=128)),
              writes=[("MWS", buf)], kind="dma")
            for j in range(9):
                col = l * 36 + piece * 9 + j
                for kc in range(8):
                    last = (piece == 3 and j == 8 and kc == 7)
                    A("pe", "matmul", dict(out=PS[7][:, col:col + 1], lhsT=MWS[buf][:, kc, j * 128:(j + 1) * 128],
                                           rhs=CONDB[:, kc:kc + 1], start=(kc == 0), stop=(kc == 7)),
                      reads=[("MWS", buf), "CONDB"], writes=[("P", 7)] + ([("MWSDONE", l)] if last else []))
        A("dve", "tensor_copy", dict(out=MODP[:, l * 36:(l + 1) * 36], in_=PS[7][:, l * 36:(l + 1) * 36]),
          reads=[("P", 7)], writes=[("MODP", l)])
        A("sp", "dma_start", dict(out=ccmin[l], in_=MODP[:, l * 36:(l + 1) * 36]), reads=[("MODP", l)], writes=[("CCMIN", l)], kind="dma")

    def mod_gather(l):
        A("pool", "collective_compute", dict(kind="AllGather", op=ALU.bypass, replica_groups=PAIRS, ins=[ccmin[l]], outs=[ccmout[l]]),
          reads=[("CCMIN", l)], writes=[("CCMOUT", l)], kind="cc")

    def mod_finalize(l):
        A("sp", "dma_start", dict(out=MOD[:, l, :].rearrange("p (r n) -> p r n", r=2), in_=ccmout[l].rearrange("(r p) n -> p r n", p=128)),
          reads=[("CCMOUT", l)], writes=[("MOD", l)], kind="dma")
        A("dve", "tensor_tensor", dict(out=MOD[:, l, :], in0=MOD[:, l, :], in1=vcol(V_MODB + l * 72, 72), op=ALU.add),
          reads=[("MOD", l), "VEC"], writes=[("MOD", l)])
        for s in range(3):
            ls = l * 3 + s
            A("dve", "scalar_tensor_tensor", dict(
                out=AV[:, ls * 8:(ls + 1) * 8], in0=MOD[:, l, (3 * s + 1) * 8:(3 * s + 2) * 8], scalar=1.0,
                in1=vcol(V_NG + ls * 8, 8), op0=ALU.add, op1=ALU.mult), reads=[("MOD", l), "VEC"], writes=[("AV", ls)])
            A("dve", "tensor_scalar", dict(
                out=GT[:, ls * 8:(ls + 1) * 8], in0=MOD[:, l, (3 * s + 2) * 8:(3 * s + 3) * 8],
                scalar1=(1.0 if s == 1 else 0.5), scalar2=None, op0=ALU.mult), reads=[("MOD", l)], writes=[("GT", ls)])

    mod_compute(0)
    mod_compute(1)

    def Hs(kc, lo, hi):
        return HB[:, kc, lo:hi]

    def emit_norm(ls, scr32, scr16, skey):
        for t in range(4):
            emit_norm_tile(ls, t, scr32, scr16, skey)

    def emit_norm_tile(ls, t, scr32, scr16, skey):
        l = ls // 3
        s = ls % 3
        if True:
            lo, hi = t * 512, (t + 1) * 512
            for kc in range(8):
                sq = scr16[:, 0, (kc % 2) * 512:(kc % 2) * 512 + 512]
                A("act", "activation", dict(out=sq, in_=XT[:, kc, lo:hi], func=AF.Square),
                  reads=[("XT", kc, t)], writes=[(skey, 0, kc % 2)])
                A("pe", "matmul", dict(out=PS[6][:, :], lhsT=ONES, rhs=sq, start=(kc == 0), stop=(kc == 7)),
                  reads=[(skey, 0, kc % 2), "CONST"], writes=[("P", 6)])
            A("act", "activation", dict(out=scr32[:, 1, :], in_=PS[6][:, :], func=AF.Sqrt, bias=vcol(V_EPS), scale=1.0 / D),
              reads=[("P", 6), "VEC"], writes=[(skey, 1)])
            A("dve", "reciprocal", dict(out=scr32[:, 2, :], in_=scr32[:, 1, :]), reads=[(skey, 1)], writes=[(skey, 2)])
            for kc in range(8):
                tb = 3 + kc % 2
                A("dve", "tensor_tensor", dict(out=scr32[:, tb, :], in0=XT[:, kc, lo:hi], in1=scr32[:, 2, :], op=ALU.mult),
                  reads=[("XT", kc, t), (skey, 2)], writes=[(skey, tb)])
                A("pool", "tensor_scalar", dict(
                    out=Hs(kc, lo, hi), in0=scr32[:, tb, :], scalar1=AV[:, ls * 8 + kc:ls * 8 + kc + 1],
                    scalar2=MOD[:, l, 3 * s * 8 + kc:3 * s * 8 + kc + 1], op0=ALU.mult, op1=ALU.add),
                  reads=[(skey, tb), ("AV", ls), ("MOD", l)], writes=[("H", kc, t)])

    gstate = {"g": 0, "pg": 0, "pd": 0}

    def ffn_load(l, w, gi):
        f0, nf = GROUPS[gi]
        buf = gstate["g"] % 2
        ncol = nf * 128
        for kc in range(8):
            A("pool", "dma_start", dict(out=WG[buf][:, kc, 0:ncol], in_=wg[l, w, kc * 128:(kc + 1) * 128, f0 * 128:f0 * 128 + ncol]),
              reads=([("MWSDONE", 0), ("MWSDONE", 1)] if (buf == 1 and kc == 0) else []), writes=[("WG", buf, kc)], kind="dma")
            A("pool", "dma_start", dict(out=WU[buf][:, kc, 0:ncol], in_=wu[l, w, kc * 128:(kc + 1) * 128, f0 * 128:f0 * 128 + ncol]),
              writes=[("WU", buf, kc)], kind="dma")
        for j in range(nf):
            A("pool", "dma_start", dict(out=WD[buf][:, j, :], in_=wd[l, w, (f0 + j) * 128:(f0 + j + 1) * 128, :]),
              writes=[("WD", buf, j)], kind="dma")
        gstate["g"] += 1
        return buf

    def ffn_flush():
        fn = gstate.pop("pending", None)
        if fn is not None:
            fn()

    def ffn_compute(ls, gi, buf, mid_hook=None):
        f0, nf = GROUPS[gi]
        for t in range(4):
            lo, hi = t * 512, (t + 1) * 512
            for j in range(nf):
                pg = gstate["pg"] % 2
                pu = 2 + gstate["pg"] % 2
                gstate["pg"] += 1
                for kc in range(8):
                    A("pe", "matmul", dict(out=PS[pg][:, :], lhsT=WG[buf][:, kc, j * 128:(j + 1) * 128],
                                           rhs=Hs(kc, lo, hi), start=(kc == 0), stop=(kc == 7)),
                      reads=[("WG", buf, kc), ("H", kc, t)], writes=[("P", pg)])
                for kc in range(8):
                    A("pe", "matmul", dict(out=PS[pu][:, :], lhsT=WU[buf][:, kc, j * 128:(j + 1) * 128],
                                           rhs=Hs(kc, lo, hi), start=(kc == 0), stop=(kc == 7)),
                      reads=[("WU", buf, kc), ("H", kc, t)], writes=[("P", pu)])
                if j == 0:
                    ffn_flush()
                sg = 5 + pg
                A("act", "activation", dict(out=SQB[:, sg, :], in_=PS[pg][:, :], func=AF.Silu),
                  reads=[("P", pg)], writes=[("S", sg)])
                A("dve", "tensor_tensor", dict(out=ACTT[:, j, :], in0=SQB[:, sg, :], in1=PS[pu][:, :], op=ALU.mult),
                  reads=[("S", sg), ("P", pu)], writes=[("ACTT", j)])
            if mid_hook is not None:
                mid_hook(t)

            def down(t=t, lo=lo, hi=hi, nf=nf, buf=buf, ls=ls):
                for oc in range(8):
                    pd = 4 + gstate["pd"] % 2
                    gstate["pd"] += 1
                    for j in range(nf):
                        A("pe", "matmul", dict(out=PS[pd][:, :], lhsT=WD[buf][:, j, oc * 128:(oc + 1) * 128],
                                               rhs=ACTT[:, j, :], start=(j == 0), stop=(j == nf - 1)),
                          reads=[("WD", buf, j), ("ACTT", j)], writes=[("P", pd)])
                    A("dve", "scalar_tensor_tensor", dict(
                        out=XT[:, oc, lo:hi], in0=PS[pd][:, :], scalar=GT[:, ls * 8 + oc:ls * 8 + oc + 1],
                        in1=XT[:, oc, lo:hi], op0=ALU.mult, op1=ALU.add),
                      reads=[("P", pd), ("GT", ls), ("XT", oc, t)], writes=[("XT", oc, t)])

            gstate["pending"] = down

    def emit_ffn(l, w, barrier=True, preloaded=None, prefetch=None, pre_last=None):
        ls = l * 3 + 2 * w
        if barrier:
            P.barrier()
        b0 = preloaded if preloaded is not None else ffn_load(l, w, 0)
        emit_norm_tile(ls, 0, SQB, SQB16, "S")
        bufs = [b0]
        nxt = None

        def hook(t):
            if t + 1 < 4:
                emit_norm_tile(ls, t + 1, SQB, SQB16, "S")

        for gi in range(4):
            ffn_flush()
            if gi + 1 < 4:
                bufs.append(ffn_load(l, w, gi + 1))
            elif prefetch is not None:
                nxt = ffn_load(prefetch[0], prefetch[1], 0)
            elif pre_last is not None:
                pre_last()
            ffn_compute(ls, gi, bufs[gi], mid_hook=(hook if gi == 0 else None))
        ffn_flush()
        return nxt

    def dump_and_finish():
        outs = []
        for kc in range(8):
            outs.append(A("sp", "dma_start", dict(out=outT[kc * 128:(kc + 1) * 128, :], in_=XT[:, kc, :]),
                          reads=[("XT", kc, t) for t in range(4)], kind="dma"))
        A("sp", None, None)
        P.ops[-1]["deps"] = outs

    BUF0_KEYS = [("WG", 0, kc) for kc in range(8)] + [("WU", 0, kc) for kc in range(8)] + [("WD", 0, j) for j in range(6)]

    def prefetch_win():
        assert gstate["g"] % 2 == 0
        for kc in range(7):
            A("pool", "dma_start", dict(out=WIN[:, kc, :], in_=hy_in[kc * 128:(kc + 1) * 128, :], max_dma_last_dim=4096),
              writes=[("WIN", kc)] + (BUF0_KEYS if kc == 0 else []), kind="dma")

    def prefetch_win1():
        assert gstate["g"] % 2 == 0
        for kc in range(6):
            A("pool", "dma_start", dict(out=WIN1[:, kc, :], in_=sc_in[kc * 128:(kc + 1) * 128, :], max_dma_last_dim=4096),
              writes=[("WIN1", kc)] + (BUF0_KEYS if kc == 0 else []), kind="dma")

    pre0 = ffn_load(0, 0, 0)
    mod_gather(0)
    mod_gather(1)
    mod_finalize(0)
    emit_ffn(0, 0, barrier=False, preloaded=pre0, pre_last=prefetch_win)
    if stop_after == "ffn00":
        P.barrier()
        dump_and_finish()
        return nc, P

    P.barrier()
    for kc in range(7, 8):
        A("pool", "dma_start", dict(out=WIN[:, kc, :], in_=hy_in[kc * 128:(kc + 1) * 128, :], max_dma_last_dim=4096),
          writes=[("WIN", kc)], kind="dma")
    mod_finalize(1)
    emit_norm(1, RS, RS16, "R")
    P.barrier()
    rot = {"b": 0}

    def nbank(n=6):
        b = rot["b"] % n
        rot["b"] += 1
        return b

    kv_dmas = []
    def kv_part(t):
        lo, hi = t * 512, (t + 1) * 512
        for c in range(4):
            pb = nbank()
            for kc in range(8):
                A("pe", "matmul", dict(out=PS[pb][:, :], lhsT=WIN[:, kc, 1536 + c * 128:1536 + (c + 1) * 128], rhs=Hs(kc, lo, hi),
                                                              start=(kc == 0), stop=(kc == 7)),
                  reads=[("WIN", kc), ("H", kc, t)], writes=[("P", pb)])
            kb = c % 2
            A("dve", "tensor_copy", dict(out=ACTT[:, kb, :], in_=PS[pb][:, :]),
              reads=[("P", pb)], writes=[("KST", kb)])
            kv_dmas.append(A("sp", "dma_start", dict(out=ccink[c * 128:(c + 1) * 128, lo:hi], in_=ACTT[:, kb, :]),
                             reads=[("KST", kb)], writes=[("CCIN", "k", c, t)], kind="dma"))
        for bl in range(4):
            n0 = lo + bl * 128
            B = n0 // 128
            pb = nbank()
            for kc in range(8):
                A("pe", "matmul", dict(out=PS[pb][:, :], lhsT=HB[:, kc, n0:n0 + 128], rhs=WIN[:, kc, 2048:2560],
                                                                 start=(kc == 0), stop=(kc == 7)),
                  reads=[("WIN", kc), ("H", kc, t)], writes=[("P", pb)])
            kb = 2 + bl % 2
            A("act", "activation", dict(out=ACTT[:, kb, :], in_=PS[pb][:, :], func=AF.Copy),
              reads=[("P", pb)], writes=[("KST", kb)])
            kv_dmas.append(A("act", "dma_start", dict(
                out=ccinv[32 * B:32 * B + 32, :].rearrange("r (q j) -> (r q) j", q=4), in_=ACTT[:, kb, :]),
                reads=[("KST", kb)], writes=[("CCIN", "v", B)], kind="dma"))

    def main_part(t):
        lo, hi = t * 512, (t + 1) * 512
        for c in range(4):
            pb = nbank()
            for kc in range(8):
                A("pe", "matmul", dict(out=PS[pb][:, :], lhsT=WIN[:, kc, c * 128:(c + 1) * 128], rhs=Hs(kc, lo, hi),
                                                              start=(kc == 0), stop=(kc == 7)),
                  reads=[("WIN", kc), ("H", kc, t)], writes=[("P", pb)])
            A("act", "activation", dict(out=RS16[:, 4 + c // 2, (c % 2) * 512:(c % 2) * 512 + 512], in_=PS[pb][:, :], func=AF.Gelu),
              reads=[("P", pb)], writes=[("UT", c)])
        VN_ = [RS16[:, 3, 0:512], RS16[:, 3, 512:1024], ACTT[:, 4, :], ACTT[:, 5, :]]
        VG_ = [RS[:, 0, :], RS[:, 1, :], RS[:, 6, :], RS[:, 7, :]]
        for bl in range(4):
            n0 = lo + bl * 128
            pb = nbank()
            for kc in range(8):
                A("pe", "matmul", dict(out=PS[pb][:, :], lhsT=HB[:, kc, n0:n0 + 128], rhs=WIN[:, kc, 512:1024],
                                       start=(kc == 0), stop=(kc == 7)),
                  reads=[("WIN", kc), ("H", kc, t)], writes=[("P", pb)])
            A("act", "activation", dict(out=VG_[bl], in_=PS[pb][:, :], func=AF.Gelu),
              reads=[("P", pb)], writes=[("VG", bl)])
            A("dve", "tensor_tensor", dict(out=RS[:, 2, :], in0=VG_[bl], in1=VG_[bl], op=ALU.mult),
              reads=[("VG", bl)], writes=[("R", 2)])
            A("dve", "tensor_reduce", dict(out=MSV[:, bl, :], in_=RS[:, 2, :].rearrange("p (h d) -> p h d", d=64), axis=AX.X, op=ALU.add),
              reads=[("R", 2)], writes=[("MSV", bl)])
        A("act", "activation", dict(out=RSV[:], in_=MSV[:], func=AF.Sqrt, bias=vcol(V_EPS), scale=1.0 / 64),
          reads=[("MSV", bl) for bl in range(4)] + ["VEC"], writes=["RSV"])
        A("dve", "reciprocal", dict(out=RV[:], in_=RSV[:]), reads=["RSV"], writes=["RV"])
        for bl in range(4):
            A("dve", "tensor_tensor", dict(out=VG_[bl], in0=VG_[bl], in1=GVN[:], op=ALU.mult),
              reads=[("VG", bl), "GVN"], writes=[("VG", bl)])
            A("dve", "tensor_tensor", dict(
                out=VN_[bl].rearrange("p (h d) -> p h d", d=64), in0=VG_[bl].rearrange("p (h d) -> p h d", d=64),
                in1=RV[:, bl, :].unsqueeze(2).broadcast_to([128, 8, 64]), op=ALU.mult),
              reads=[("VG", bl), "RV"], writes=[("VN", bl)])
        for c in range(4):
            pb = nbank()
            for kc in range(8):
                A("pe", "matmul", dict(out=PS[pb][:, :], lhsT=WIN[:, kc, 1024 + c * 128:1024 + (c + 1) * 128], rhs=Hs(kc, lo, hi),
                                                              start=(kc == 0), stop=(kc == 7)),
                  reads=[("WIN", kc), ("H", kc, t)], writes=[("P", pb)])
            A("act", "activation", dict(out=QT[:, c, lo:hi], in_=PS[pb][:, :], func=AF.Copy),
              reads=[("P", pb)], writes=[("QT", c, t)])
        for bl in range(4):
            n0 = lo + bl * 128
            for c in range(4):
                sb = 6 + (c % 2)
                for hh in range(2):
                    A("pe", "matmul", dict(
                        out=PS[sb][:, hh * 128:(hh + 1) * 128], lhsT=VN_[bl][:, c * 128:(c + 1) * 128],
                        rhs=WCT[:, (2 * c + hh) * 128:(2 * c + hh + 1) * 128], start=True, stop=True),
                      reads=[("VN", bl), "WCT"], writes=[("P", sb)])
                tb = c % 2
                for hh in range(2):
                    A("dve", "tensor_tensor", dict(
                        out=T1[hh * 64:(hh + 1) * 64, tb, :], in0=PS[sb][hh * 64:(hh + 1) * 64, hh * 128:(hh + 1) * 128],
                        in1=BIAS[hh * 64:(hh + 1) * 64, c * 128:(c + 1) * 128], op=ALU.add),
                      reads=[("P", sb), "BIAS"], writes=[("R6", tb, hh)])
                A("pool", "tensor_tensor", dict(
                    out=YA[:, c, n0:n0 + 128], in0=T1[:, tb, :],
                    in1=RS16[:, 4 + c // 2, (c % 2) * 512 + bl * 128:(c % 2) * 512 + (bl + 1) * 128], op=ALU.mult),
                  reads=[("R6", tb, 0), ("R6", tb, 1), ("UT", c)], writes=[("YA", c, t)])
    for t in range(4):
        kv_part(t)
    cck = A("pool", "collective_compute", dict(kind="AllGather", op=ALU.bypass, replica_groups=PAIRS, ins=[ccink], outs=[ccoutk]),
            reads=[], writes=["CCOUTK"], kind="cc")
    P.ops[cck]["deps"] = list(set(P.ops[cck]["deps"]) | set(kv_dmas))
    ccv = A("pool", "collective_compute", dict(kind="AllGather", op=ALU.bypass, replica_groups=PAIRS, ins=[ccinv], outs=[ccoutv]),
            reads=[], writes=["CCOUTV"], kind="cc")
    P.ops[ccv]["deps"] = list(set(P.ops[ccv]["deps"]) | set(kv_dmas))
    for t in range(4):
        main_part(t)
    if stop_after == "inproj":
        P.barrier()
        dump_and_finish()
        return nc, P
    P.barrier()
    A("sp", "dma_start", dict(out=VALL[:, 0:16, :], in_=ccoutv[0:512, :].rearrange("r (q j) -> (r q) j", q=4).rearrange("(b p) j -> p b j", p=128)),
      reads=["CCOUTV"], writes=["VO"], kind="dma")
    A("act", "dma_start", dict(out=VALL[:, 16:32, :], in_=ccinv[0:512, :].rearrange("r (q j) -> (r q) j", q=4).rearrange("(b p) j -> p b j", p=128)),
      reads=[], writes=["VOWN"], kind="dma")
    for i in range(4):
        A("dve", "tensor_scalar", dict(out=VALL[:, 4 * i:4 * i + 4, :], in0=VALL[:, 4 * i:4 * i + 4, :], scalar1=vcol(V_A), scalar2=None, op0=ALU.mult),
          reads=["VO", "VEC"], writes=["VO"])

    if stop_after == "cc":
        P.barrier()
        dump_and_finish()
        return nc, P
    def kts_load(c):
        b = c % 2
        A("sp", "dma_start", dict(out=KTS[b][:, 0:2048], in_=ccoutk[c * 128:(c + 1) * 128, :]), reads=["CCOUTK"], writes=[("KTS", b, 0)], kind="dma")
        A("act", "dma_start", dict(out=KTS[b][:, 2048:4096], in_=ccink[c * 128:(c + 1) * 128, :]), reads=[], writes=[("KTS", b, 1)], kind="dma")

    def v2(ap):
        return ap.rearrange("p (h n) -> p h n", h=2)

    E_ = [WK8[:, 0:2, :], WK8[:, 2:4, :], RS[:, 0:2, :]]
    G_ = [v2(RS16[:, 2, :]), v2(RS16[:, 3, :])]
    SP_ = [v2(RS16[:, 4, :]), v2(RS16[:, 5, :])]
    W_ = [v2(RS16[:, 6, :]), v2(RS16[:, 7, :])]
    SR_ = [ACTT[:, 0:2, :], ACTT[:, 2:4, :]]
    MASK2 = MASKT.unsqueeze(1).broadcast_to([128, 2, 128])

    units = []
    for c in range(4):
        for g in range(4):
            blocks = [("own", i) for i in range(4 * g + 3, -1, -1)] + [("oth", i) for i in range(15, -1, -1)]
            for bi, (kind, i) in enumerate(blocks):
                diag = kind == "own" and i >= 4 * g
                c0 = 128 * (i - 4 * g) if diag else 0
                units.append(dict(c=c, g=g, kind=kind, i=i, diag=diag, c0=c0, first=(bi == 0), last=(bi == len(blocks) - 1),
                                  bi=bi, pg=c * 4 + g))
    kts_load(0)

    def QK(u, n):
        c, g, c0 = u["c"], u["g"], u["c0"]
        zb = 0 if n % 2 == 0 else 6
        kcol = (2048 if u["kind"] == "own" else 0) + u["i"] * 128
        q0 = g * 512 + c0
        q1 = (g + 1) * 512
        kb = c % 2
        ksl = 1 if u["kind"] == "own" else 0
        if u["first"] and g == 0 and c + 1 < 4:
            kts_load(c + 1)
        for hh in range(2):
            A("pe", "matmul", dict(out=PS[zb + hh][:, c0:512], lhsT=KTS[kb][hh * 64:(hh + 1) * 64, kcol:kcol + 128],
                                   rhs=QT[hh * 64:(hh + 1) * 64, c, q0:q1], start=True, stop=True),
              reads=[("KTS", kb, ksl), ("QT", c, g)], writes=[("P", zb + hh)])

    def S1a(u, n):
        c0 = u["c0"]
        eb = n % 3
        zb = 0 if n % 2 == 0 else 6
        A("act", "activation", dict(out=E_[eb][:, :, c0:512], in_=P2(zb)[:, :, c0:512], func=AF.Exp, scale=0.125),
          reads=[("P", zb), ("P", zb + 1)], writes=[("E", eb)])

    def S1b(u, n):
        c0 = u["c0"]
        eb = n % 3
        sb = n % 2
        A("act", "activation", dict(out=SP_[sb][:, :, c0:512], in_=E_[eb][:, :, c0:512], func=AF.Ln, bias=vcol(V_ONE), scale=1.0),
          reads=[("E", eb), "VEC"], writes=[("SP", sb)])
        if u["diag"]:
            A("pool", "tensor_tensor", dict(out=SP_[sb][:, :, c0:c0 + 128], in0=SP_[sb][:, :, c0:c0 + 128], in1=MASK2, op=ALU.mult),
              reads=[("SP", sb), "CONST"], writes=[("SP", sb)])

    def S2a(u, n):
        c0 = u["c0"]
        sb = n % 2
        cur = u["bi"] % 2
        nxt = (u["bi"] + 1) % 2
        if u["first"]:
            for i in range(2):
                A("pool", "memset", dict(ap=SR_[i], constant=0.0), writes=[("SR", i)])
        for hh in range(2):
            A("pe", "matmul", dict(out=PS[2 + hh][:, c0:512], lhsT=TRI, rhs=SP_[sb][:, hh, c0:512], start=True, stop=u["first"]),
              reads=[("SP", sb), "CONST"], writes=[("P", 2 + hh)])
            if not u["first"]:
                A("pe", "matmul", dict(out=PS[2 + hh][:, c0:512], lhsT=ONES, rhs=SR_[cur][:, hh, c0:512], start=False, stop=True),
                  reads=[("SR", cur), "CONST"], writes=[("P", 2 + hh)])
        if not u["last"]:
            A("dve", "tensor_tensor", dict(out=SR_[nxt][:, :, c0:512], in0=SR_[cur][:, :, c0:512], in1=SP_[sb][:, :, c0:512], op=ALU.add),
              reads=[("SR", cur), ("SP", sb)], writes=[("SR", nxt)])

    def S2b(u, n):
        c0 = u["c0"]
        gb = n % 2
        A("act", "activation", dict(out=G_[gb][:, :, c0:512], in_=P2(2)[:, :, c0:512], func=AF.Exp, scale=-1.0),
          reads=[("P", 2), ("P", 3)], writes=[("G", gb)])

    def S3(u, n):
        c, g, c0 = u["c"], u["g"], u["c0"]
        eb = n % 3
        gb = n % 2
        wb = n % 2
        vblk = (16 if u["kind"] == "own" else 0) + u["i"]
        A("dve", "tensor_tensor", dict(out=W_[wb][:, :, c0:512], in0=E_[eb][:, :, c0:512], in1=G_[gb][:, :, c0:512], op=ALU.mult),
          reads=[("E", eb), ("G", gb)], writes=[("W", wb)])
        if u["diag"]:
            A("pool", "tensor_tensor", dict(out=W_[wb][:, :, c0:c0 + 128], in0=W_[wb][:, :, c0:c0 + 128], in1=MASK2, op=ALU.mult),
              reads=[("W", wb), "CONST"], writes=[("W", wb)])
        for hh in range(2):
            yb = 4 + hh
            h = 2 * c + hh
            if u["first"]:
                A("pe", "matmul", dict(out=PS[yb][0:64, :], lhsT=ZEROS[:, 0:64], rhs=VALL[:, 16, :], start=True, stop=False),
                  reads=["CONST", "VOWN"], writes=[("P", yb)])
            A("pe", "matmul", dict(out=PS[yb][0:64, c0:512], lhsT=VALL[:, vblk, h * 64:(h + 1) * 64], rhs=W_[wb][:, hh, c0:512],
                                   start=False, stop=u["last"]),
              reads=[("W", wb), "VO" if u["kind"] == "oth" else "VOWN"], writes=[("P", yb)])
            if u["last"]:
                A("act", "activation", dict(out=YB[hh * 64:(hh + 1) * 64, c, g * 512:(g + 1) * 512], in_=PS[yb][0:64, :], func=AF.Copy),
                  reads=[("P", yb)], writes=[("YB", c, g, hh)])

    NU = len(units)
    QK(units[0], 0)
    for n in range(NU + 2):
        if n + 1 < NU:
            QK(units[n + 1], n + 1)
        if n < NU:
            S1a(units[n], n)
        if 0 <= n - 2 < NU:
            S2b(units[n - 2], n - 2)
        if 0 <= n - 1 < NU:
            S2a(units[n - 1], n - 1)
        if n < NU:
            S1b(units[n], n)
        if 0 <= n - 2 < NU:
            S3(units[n - 2], n - 2)

    if stop_after == "attn":
        P.barrier()
        for kc in range(8):
            for t in range(4):
                src = YA[:, kc, t * 512:(t + 1) * 512] if kc < 4 else YB[:, kc - 4, t * 512:(t + 1) * 512]
                A("dve", "tensor_copy", dict(out=XT[:, kc, t * 512:(t + 1) * 512], in_=src), reads=[], writes=[("XT", kc, t)])
        P.barrier()
        dump_and_finish()
        return nc, P
    P.barrier()
    for kc in range(8):
        A("pool", "dma_start", dict(out=WO[:, kc, :], in_=hy_out[kc * 128:(kc + 1) * 128, :]), writes=[("WO", kc)], kind="dma")
    for t in range(4):
        lo, hi = t * 512, (t + 1) * 512
        for oc in range(8):
            pb = nbank()
            for kc in range(8):
                src = YA[:, kc, lo:hi] if kc < 4 else YB[:, kc - 4, lo:hi]
                rk = [("YA", kc, t)] if kc < 4 else [("YB", kc - 4, t, 0), ("YB", kc - 4, t, 1)]
                A("pe", "matmul", dict(out=PS[pb][:, :], lhsT=WO[:, kc, oc * 128:(oc + 1) * 128], rhs=src,
                                                                          start=(kc == 0), stop=(kc == 7)),
                  reads=[("WO", kc)] + rk, writes=[("P", pb)])
            A("dve", "scalar_tensor_tensor", dict(
                out=XT[:, oc, lo:hi], in0=PS[pb][:, :], scalar=GT[:, 8 + oc:8 + oc + 1], in1=XT[:, oc, lo:hi], op0=ALU.mult, op1=ALU.add),
              reads=[("P", pb), ("GT", 1), ("XT", oc, t)], writes=[("XT", oc, t)])
    if stop_after == "mix0":
        P.barrier()
        dump_and_finish()
        return nc, P

    pre = emit_ffn(0, 1, prefetch=(1, 0))
    emit_ffn(1, 0, barrier=False, preloaded=pre, pre_last=prefetch_win1)
    if stop_after == "ffn10":
        P.barrier()
        dump_and_finish()
        return nc, P

    P.barrier()
    for kc in range(6, 8):
        A("pool", "dma_start", dict(out=WIN1[:, kc, :], in_=sc_in[kc * 128:(kc + 1) * 128, :], max_dma_last_dim=4096), writes=[("WIN1", kc)], kind="dma")
    for kc in range(8):
        A("pool", "dma_start", dict(out=WO1[:, kc, :], in_=sc_out[kc * 128:(kc + 1) * 128, :]), writes=[("WO1", kc)], kind="dma")
    emit_norm(4, SQB, SQB16, "S")
    P.barrier()
    for c in range(8):
        for which in range(2):
            colw = 1024 * (1 + which) + c * 128
            for kc in range(8):
                A("pe", "matmul", dict(
                    out=PS[7][:, which * 16 + 2 * c:which * 16 + 2 * c + 2], lhsT=WIN1[:, kc, colw:colw + 128], rhs=HB[:, kc, NT - 2:NT],
                    start=(kc == 0), stop=(kc == 7)), reads=[("WIN1", kc), ("H", kc, 3)], writes=[("P", 7)])
    A("act", "activation", dict(out=TAIL[:, 16:32], in_=PS[7][:, 0:16], func=AF.Copy), reads=[("P", 7)], writes=["TAILc"])
    A("dve", "tensor_tensor", dict(out=TAIL[:, 0:16], in0=TAIL[:, 16:32], in1=PS[7][:, 16:32], op=ALU.mult), reads=["TAILc", ("P", 7)], writes=["TAIL"])
    A("sp", "dma_start", dict(out=cc2in, in_=TAIL[:, 0:16]), reads=["TAIL"], writes=["CC2IN"], kind="dma")
    A("pool", "collective_compute", dict(kind="AllGather", op=ALU.bypass, replica_groups=PAIRS, ins=[cc2in], outs=[cc2out]),
      reads=["CC2IN"], writes=["CC2OUT"], kind="cc")
    A("sp", "dma_start", dict(out=HALOP[:], in_=cc2out[0:128, :]), reads=["CC2OUT"], writes=["HALOP"], kind="dma")
    A("dve", "tensor_scalar", dict(out=HALOP[:], in0=HALOP[:], scalar1=vcol(V_A), scalar2=None, op0=ALU.mult), reads=["HALOP", "VEC"], writes=["HALOP"])
    CXHt = T("CXH", [128, 1028], F32, O_QB + 2048)
    for t in range(4):
        lo, hi = t * 512, (t + 1) * 512
        for c in range(8):
            pbk = [(c % 2) * 3 + i for i in range(3)]
            for which in range(3):
                colw = 1024 * which + c * 128
                for kc in range(8):
                    A("pe", "matmul", dict(
                        out=PS[pbk[which]][:, :], lhsT=WIN1[:, kc, colw:colw + 128], rhs=Hs(kc, lo, hi), start=(kc == 0), stop=(kc == 7)),
                      reads=[("WIN1", kc), ("H", kc, t)], writes=[("P", pbk[which])])
            cxo = (c % 2) * 514
            A("act", "activation", dict(out=SQB[:, 0, :], in_=PS[pbk[1]][:, :], func=AF.Copy), reads=[("P", pbk[1])], writes=[("S", 0)])
            A("dve", "tensor_tensor", dict(out=CXHt[:, cxo + 2:cxo + 514], in0=SQB[:, 0, :], in1=PS[pbk[2]][:, :], op=ALU.mult),
              reads=[("S", 0), ("P", pbk[2])], writes=[("CXH", c % 2, 1)])
            if t == 0:
                A("pool", "tensor_copy", dict(out=CXHt[:, cxo:cxo + 2], in_=HALOP[:, 2 * c:2 * c + 2]), reads=["HALOP"], writes=[("CXH", c % 2, 0)])
            else:
                A("pool", "tensor_copy", dict(out=CXHt[:, cxo:cxo + 2], in_=PREV[:, 2 * c:2 * c + 2]), reads=[("PREV", c)], writes=[("CXH", c % 2, 0)])
            A("pool", "tensor_copy", dict(out=PREV[:, 2 * c:2 * c + 2], in_=CXHt[:, cxo + 512:cxo + 514]), reads=[("CXH", c % 2, 1)], writes=[("PREV", c)])
            yk = [("CXH", c % 2, 0), ("CXH", c % 2, 1), "VEC"]
            A("dve", "tensor_scalar", dict(out=SQB[:, 5, :], in0=CXHt[:, cxo + 2:cxo + 514], scalar1=vcol(V_CW + 16 + c), scalar2=None, op0=ALU.mult),
              reads=yk, writes=[("S", 5)])
            A("dve", "scalar_tensor_tensor", dict(out=SQB[:, 5, :], in0=CXHt[:, cxo + 1:cxo + 513], scalar=vcol(V_CW + 8 + c), in1=SQB[:, 5, :], op0=ALU.mult, op1=ALU.add),
              reads=yk + [("S", 5)], writes=[("S", 5)])
            A("dve", "scalar_tensor_tensor", dict(out=SQB[:, 5, :], in0=CXHt[:, cxo:cxo + 512], scalar=vcol(V_CW + c), in1=SQB[:, 5, :], op0=ALU.mult, op1=ALU.add),
              reads=yk + [("S", 5)], writes=[("S", 5)])
            A("dve", "tensor_tensor", dict(out=YC[:, c, :], in0=SQB[:, 5, :], in1=PS[pbk[0]][:, :], op=ALU.mult),
              reads=[("S", 5), ("P", pbk[0])], writes=[("YC", c)])
        for oc in range(8):
            pb = 6 + oc % 2
            for kc in range(8):
                A("pe", "matmul", dict(out=PS[pb][:, :], lhsT=WO1[:, kc, oc * 128:(oc + 1) * 128], rhs=YC[:, kc, :],
                                                                start=(kc == 0), stop=(kc == 7)),
                  reads=[("WO1", kc), ("YC", kc)], writes=[("P", pb)])
            A("dve", "scalar_tensor_tensor", dict(
                out=XT[:, oc, lo:hi], in0=PS[pb][:, :], scalar=GT[:, 32 + oc:32 + oc + 1], in1=XT[:, oc, lo:hi], op0=ALU.mult, op1=ALU.add),
              reads=[("P", pb), ("GT", 4), ("XT", oc, t)], writes=[("XT", oc, t)])
    if stop_after == "mix1":
        P.barrier()
        dump_and_finish()
        return nc, P

    emit_ffn(1, 1)

    P.barrier()
    outs = []
    for t in range(4):
        lo, hi = t * 512, (t + 1) * 512
        for kc in range(8):
            sq = SQB16[:, 0, (kc % 2) * 512:(kc % 2) * 512 + 512]
            A("act", "activation", dict(out=sq, in_=XT[:, kc, lo:hi], func=AF.Square), reads=[("XT", kc, t)], writes=[("S", 0, kc % 2)])
            A("pe", "matmul", dict(out=PS[6][:, :], lhsT=ONES, rhs=sq, start=(kc == 0), stop=(kc == 7)),
              reads=[("S", 0, kc % 2), "CONST"], writes=[("P", 6)])
        A("act", "activation", dict(out=SQB[:, 1, :], in_=PS[6][:, :], func=AF.Sqrt, bias=vcol(V_EPS), scale=1.0 / D), reads=[("P", 6), "VEC"], writes=[("S", 1)])
        A("dve", "reciprocal", dict(out=SQB[:, 2, :], in_=SQB[:, 1, :]), reads=[("S", 1)], writes=[("S", 2)])
        for kc in range(8):
            ob = 3 + kc % 4
            A("dve", "scalar_tensor_tensor", dict(out=SQB[:, ob, :], in0=XT[:, kc, lo:hi], scalar=vcol(V_FG + kc), in1=SQB[:, 2, :], op0=ALU.mult, op1=ALU.mult),
              reads=[("XT", kc, t), ("S", 2), "VEC"], writes=[("S", ob)])
            q = "sp" if kc % 2 == 0 else "act"
            outs.append(A(q, "dma_start", dict(out=outT[kc * 128:(kc + 1) * 128, lo:hi], in_=SQB[:, ob, :]), reads=[("S", ob)], kind="dma"))
    A("sp", None, None)
    P.ops[-1]["deps"] = outs
    return nc, P


_CACHE = {}


def get_nc(stop_after=None):
    if stop_after in _CACHE:
        return _CACHE[stop_after]
    nc, P = build_nc(stop_after)
    sems = {k: nc.alloc_semaphore(name="s_" + k) for k in ("act", "dve", "pool", "pe")}
    sems["async"] = [nc.alloc_semaphore(name=f"s_as{i}") for i in range(48)]
    P.prepare(48)
    with nc.Block() as block:
        @block.sync
        def _(e):
            P.emit_engine(sems, "sp", e)

        @block.scalar
        def _(e):
            P.emit_engine(sems, "act", e)

        @block.vector
        def _(e):
            P.emit_engine(sems, "dve", e)

        @block.gpsimd
        def _(e):
            P.emit_engine(sems, "pool", e)

        @block.tensor
        def _(e):
            P.emit_engine(sems, "pe", e)
    _CACHE[stop_after] = nc
    return nc


def make_inputs(x, c, mod_w, mod_b, norm_g, ffn_w_gate, ffn_w_up, ffn_w_down, hy_w_in, hy_w_out, gm_vnorm_g, gm_w_s, gm_b_s,
                sc_w_in, sc_conv_w, sc_w_out, final_norm_g):
    f = np.float32
    x = np.asarray(x, f); c = np.asarray(c, f)
    idx = np.arange(128)
    constb = np.zeros((128, 768), f)
    constb[:, 0:128] = 1.0
    constb[:, 128:256] = (idx[:, None] >= idx[None, :])
    constb[:, 256:384] = (idx[:, None] < idx[None, :])
    constb[:, 384:512] = (idx[:, None] <= idx[None, :])
    gvn = np.ascontiguousarray(np.broadcast_to(np.asarray(gm_vnorm_g, f)[0][None, :], (128, 512)))
    bs = np.asarray(gm_b_s, f)[0]
    gbias = np.zeros((128, 4, 128), f)
    for cc in range(4):
        gbias[0:64, cc, :] = bs[2 * cc][None, :]
        gbias[64:128, cc, :] = bs[2 * cc + 1][None, :]
    gbias = gbias.reshape(128, 512)
    ws = np.asarray(gm_w_s, f)[0]
    wsT = np.ascontiguousarray(ws.transpose(2, 0, 1)).reshape(128, 1024)
    base = dict(
        constb=constb, gvn=gvn, gbias=gbias, wsT=wsT,
        wg=np.ascontiguousarray(ffn_w_gate, f), wu=np.ascontiguousarray(ffn_w_up, f),
        wd=np.ascontiguousarray(ffn_w_down, f), hy_in=np.ascontiguousarray(np.asarray(hy_w_in, f)[0]),
        hy_out=np.ascontiguousarray(np.asarray(hy_w_out, f)[0]), sc_in=np.ascontiguousarray(np.asarray(sc_w_in, f)[0]),
        sc_out=np.ascontiguousarray(np.asarray(sc_w_out, f)[0]))
    mod_b = np.asarray(mod_b, f); norm_g = np.asarray(norm_g, f); mod_w = np.asarray(mod_w, f)
    cw = np.asarray(sc_conv_w, f)[0]
    fg = np.asarray(final_norm_g, f)
    in_maps = []
    for core in range(8):
        b, r = core // 2, core % 2
        vec = np.zeros((128, NV), f)
        for l in range(2):
            vec[:, V_MODB + l * 72:V_MODB + (l + 1) * 72] = mod_b[l].reshape(72, 128).T
            for s in range(3):
                ls = l * 3 + s
                vec[:, V_NG + ls * 8:V_NG + (ls + 1) * 8] = norm_g[l, s].reshape(8, 128).T
        vec[:, V_FG:V_FG + 8] = fg.reshape(8, 128).T
        for k in range(3):
            vec[:, V_CW + k * 8:V_CW + (k + 1) * 8] = cw[k].reshape(8, 128).T
        vec[:, V_A] = float(r)
        vec[:, V_EPS] = EPS
        vec[:, V_ONE] = 1.0
        vec[:, V_C:V_C + 8] = c[b].reshape(8, 128).T
        m = dict(base)
        m["xT"] = np.ascontiguousarray(x[b, r * NT:(r + 1) * NT, :].T)
        m["vec"] = vec
        m["modw"] = np.ascontiguousarray(mod_w[:, :, r * 4608:(r + 1) * 4608])
        in_maps.append(m)
    return in_maps


def kernel(**inputs):
    nc = get_nc(STOP_AFTER)
    in_maps = make_inputs(**inputs)
    res = run_bass_kernel_spmd(nc, in_maps, core_ids=list(range(8)))
    out = np.empty((4, 2 * NT, D), np.float32)
    for core in range(8):
        b, r = core // 2, core % 2
        out[b, r * NT:(r + 1) * NT, :] = res.results[core]["outT"].T
    return out
```

```python
import numpy as np
import concourse.bass as bass
import concourse.mybir as mybir
from concourse.bass_utils import run_bass_kernel_spmd

F32 = mybir.dt.float32
BF16 = mybir.dt.bfloat16
AF = mybir.ActivationFunctionType
ALU = mybir.AluOpType
AX = mybir.AxisListType

NT = 2048
D = 1024
DFF = 2816
GROUPS = [(0, 6), (6, 6), (12, 5), (17, 5)]
EPS = 1e-6
PAIRS = [[0, 1], [2, 3], [4, 5], [6, 7]]

V_MODB = 0
V_NG = 144
V_FG = 192
V_CW = 200
V_A = 224
V_EPS = 225
V_ONE = 226
V_C = 227
V_OH = 259
NV = 263
ALL8 = [[0, 1, 2, 3, 4, 5, 6, 7]]

STOP_AFTER = None


class Prog:
    COMPUTE = ("act", "dve", "pool", "pe")

    def __init__(self):
        self.ops = []
        self.tiles = {}
        self.last = {}
        self.bar = None
        self.bar_seen = set()
        self.async_since_bar = []

    def add(self, eng, fn, kw=None, reads=(), writes=(), kind="c"):
        idx = len(self.ops)
        deps = set()
        for k in reads:
            t = self.tiles.get(k)
            if t is not None and t["w"] is not None:
                deps.add(t["w"])
        for k in writes:
            t = self.tiles.get(k)
            if t is not None:
                if t["w"] is not None:
                    deps.add(t["w"])
                deps.update(t["r"].values())
                deps.update(t["ra"])
        if self.bar is not None and eng not in self.bar_seen:
            deps.update(self.bar)
            self.bar_seen.add(eng)
        for k in reads:
            t = self.tiles.setdefault(k, {"w": None, "r": {}, "ra": []})
            if kind == "c":
                t["r"][eng] = idx
            else:
                t["ra"].append(idx)
        for k in writes:
            self.tiles[k] = {"w": idx, "r": {}, "ra": []}
        red = {}
        adeps = []
        for d in deps:
            o = self.ops[d]
            if o["kind"] == "c":
                if o["eng"] == "pe" and eng == "pe" and kind == "c":
                    continue
                if o["eng"] not in red or red[o["eng"]] < d:
                    red[o["eng"]] = d
            else:
                adeps.append(d)
        self.ops.append(dict(eng=eng, fn=fn, kw=kw, deps=list(red.values()) + adeps, kind=kind))
        if kind == "c":
            self.last[eng] = idx
        else:
            self.async_since_bar.append(idx)
        return idx

    def barrier(self):
        self.bar = set(self.last.values()) | set(self.async_since_bar)
        self.bar_seen = set()
        self.async_since_bar = []

    def prepare(self, n_async):
        ops = self.ops
        needed = [False] * len(ops)
        for o in ops:
            for d in o["deps"]:
                needed[d] = True
        cnt = {e: 0 for e in self.COMPUTE}
        ause = [0] * n_async
        aprev = [None] * n_async
        k = 0
        for i, o in enumerate(ops):
            if o["kind"] == "c":
                if needed[i]:
                    cnt[o["eng"]] += 1
                    o["ms"] = cnt[o["eng"]]
                else:
                    o["ms"] = None
            else:
                j = k % n_async
                k += 1
                inc = 16 if o["kind"] == "dma" else 1
                o["prev"] = aprev[j]
                ause[j] += inc
                o["sem"] = j
                o["val"] = ause[j]
                o["inc"] = inc
                aprev[j] = i

    def emit_engine(self, sems, ename, eng):
        ops = self.ops
        waited = {}

        def wait(sem_key, sem, val):
            if waited.get(sem_key, 0) >= val:
                return
            waited[sem_key] = val
            eng.wait_ge(sem, val)

        for i, o in enumerate(ops):
            if o["eng"] != ename:
                continue
            for d in o["deps"]:
                od = ops[d]
                if od["kind"] == "c":
                    wait(od["eng"], sems[od["eng"]], od["ms"])
                else:
                    wait(("a", od["sem"]), sems["async"][od["sem"]], od["val"])
            if o["kind"] != "c" and o["prev"] is not None:
                op = ops[o["prev"]]
                wait(("a", op["sem"]), sems["async"][op["sem"]], op["val"])
            if o["fn"] is None:
                continue
            ins = getattr(eng, o["fn"])(**o["kw"])
            if o["kind"] == "c":
                if o["ms"] is not None:
                    ins.then_inc(sems[ename], 1)
            else:
                ins.then_inc(sems["async"][o["sem"]], o["inc"])


def build_nc(stop_after=None):
    nc = bass.Bass("TRN2", target_bir_lowering=False)
    dt = nc.dram_tensor
    xT = dt("xT", [D, NT], F32, kind="ExternalInput").ap()
    vec = dt("vec", [128, NV], F32, kind="ExternalInput").ap()
    constb = dt("constb", [128, 768], F32, kind="ExternalInput").ap()
    gvn_d = dt("gvn", [128, 512], F32, kind="ExternalInput").ap()
    bias_d = dt("gbias", [128, 512], F32, kind="ExternalInput").ap()
    wsT_d = dt("wsT", [128, 1024], F32, kind="ExternalInput").ap()
    modw = dt("modw", [2, D, 4608], F32, kind="ExternalInput").ap()
    ccmin = [dt(f"ccmin{l}", [128, 36], F32, kind="Internal").ap() for l in range(2)]
    ccmout = [dt(f"ccmout{l}", [256, 36], F32, kind="Internal").ap() for l in range(2)]
    wg = dt("wg", [2, 2, D, DFF], F32, kind="ExternalInput").ap()
    wu = dt("wu", [2, 2, D, DFF], F32, kind="ExternalInput").ap()
    wd = dt("wd", [2, 2, DFF, D], F32, kind="ExternalInput").ap()
    hy_in = dt("hy_in", [D, 2560], F32, kind="ExternalInput").ap()
    hy_out = dt("hy_out", [D, D], F32, kind="ExternalInput").ap()
    sc_in = dt("sc_in", [D, 3072], F32, kind="ExternalInput").ap()
    sc_out = dt("sc_out", [D, D], F32, kind="ExternalInput").ap()
    outT = dt("outT", [D, NT], F32, kind="ExternalOutput").ap()
    ccink = dt("ccink", [512, 2048], BF16, kind="Internal").ap()
    ccoutk = dt("ccoutk", [1024, 2048], BF16, kind="Internal").ap()
    ccinv = dt("ccinv", [512, 2048], BF16, kind="Internal").ap()
    ccoutv = dt("ccoutv", [1024, 2048], BF16, kind="Internal").ap()
    cc2in = dt("cc2in", [128, 16], F32, kind="Internal").ap()
    cc2out = dt("cc2out", [256, 16], F32, kind="Internal").ap()

    TOTAL = 206 * 1024
    beg, end = nc.bump_sbuf(TOTAL)
    at = nc.alloc_sbuf_tensor_at
    O_XT, O_HB, O_QB, O_RING, O_SMALL = 0, 65536, 98304, 114688, 188416

    def T(name, shape, dtype, off):
        return at(name, shape, dtype, offset=beg + off)

    XT = T("XT", [128, 8, NT], F32, O_XT)
    HB = T("HB", [128, 8, NT], BF16, O_HB)
    VALL = T("VALL", [128, 32, 512], BF16, O_HB)
    QT = T("QT", [128, 4, NT], BF16, O_QB)
    SQB = [T(f"SQB{i}", [128, 8, 512], F32, O_QB) for i in range(1)][0]
    SQB16 = T("SQB16", [128, 8, 1024], BF16, O_QB)
    WG = [T(f"WG{b}", [128, 8, 768], BF16, O_RING + b * 36864) for b in range(2)]
    WU = [T(f"WU{b}", [128, 8, 768], BF16, O_RING + b * 36864 + 12288) for b in range(2)]
    WD = [T(f"WD{b}", [128, 6, 1024], BF16, O_RING + b * 36864 + 24576) for b in range(2)]
    MWS = [T(f"MWS{l}", [128, 8, 1152], BF16, O_RING + 36864 + l * 18432) for l in range(2)]
    WIN = T("WIN", [128, 8, 2560], BF16, O_RING)
    KTS = [T(f"KTS{b}", [128, 4096], BF16, O_RING + b * 8192) for b in range(2)]
    WO = T("WO", [128, 8, 1024], BF16, O_RING)
    YB = T("YB", [128, 4, NT], BF16, O_RING + 16384)
    WK8 = T("WK8", [128, 4, 512], F32, O_RING + 32768)
    WK8b = T("WK8b", [128, 4, 1024], BF16, O_RING + 32768)
    YA = T("YA", [128, 4, NT], BF16, O_RING + 40960)
    RS = T("RS", [128, 8, 512], F32, O_RING + 57344)
    RS16 = T("RS16", [128, 8, 1024], BF16, O_RING + 57344)
    WIN1 = T("WIN1", [128, 8, 3072], BF16, O_RING)
    WO1 = T("WO1", [128, 8, 1024], BF16, O_RING + 49152)
    YC = T("YC", [128, 8, 512], BF16, O_RING + 65536)
    o = O_SMALL
    sm = {}

    def SM(name, shape, dtype, nbytes):
        nonlocal o
        t = T(name, shape, dtype, o)
        o += (nbytes + 31) // 32 * 32
        return t

    CONSTB = SM("CONSTB", [128, 768], BF16, 1536)
    VEC = SM("VEC", [128, NV], F32, NV * 4)
    MOD = SM("MOD", [128, 2, 72], F32, 576)
    AV = SM("AV", [128, 48], F32, 192)
    GT = SM("GT", [128, 48], F32, 192)
    CONDB = SM("CONDB", [128, 32], BF16, 64)
    CONDF = SM("CONDF", [128, 32], F32, 128)
    RAW = [SM(f"RAW{l}", [128, 8, 36], F32, 1152) for l in range(2)]
    MODP = SM("MODP", [128, 72], F32, 288)
    BIAS = SM("BIAS", [128, 512], F32, 2048)
    GVN = SM("GVN", [128, 512], F32, 2048)
    WCT = SM("WCT", [128, 1024], BF16, 2048)
    MSV = SM("MSV", [128, 4, 8], F32, 128)
    RSV = SM("RSV", [128, 4, 8], F32, 128)
    RV = SM("RV", [128, 4, 8], F32, 128)
    TAIL = SM("TAIL", [128, 32], F32, 128)
    HALOP = SM("HALOP", [128, 16], F32, 64)
    PREV = SM("PREV", [128, 16], F32, 64)
    ACTT = SM("ACTT", [128, 6, 512], BF16, 6144)
    T1 = SM("T1", [128, 2, 128], F32, 1024)
    assert o <= TOTAL, o

    ONES = CONSTB[:, 0:128]
    TRI = CONSTB[:, 128:256]
    MASKT = CONSTB[:, 256:384]
    MASKLE = CONSTB[:, 384:512]
    ZEROS = CONSTB[:, 512:640]

    def vcol(c, n=1):
        return VEC[:, c:c + n]

    PSA = nc.alloc_psum_tensor("psa", [128, 4096], F32)
    PS = [PSA[:, i * 512:(i + 1) * 512] for i in range(8)]

    def P2(i):
        return PSA[:, i * 512:(i + 2) * 512].rearrange("p (h n) -> p h n", h=2)

    P = Prog()
    A = P.add

    A("sp", "dma_start", dict(out=VEC[:], in_=vec), writes=["VEC"], kind="dma")
    A("pool", "dma_start", dict(out=CONSTB[:], in_=constb), writes=["CONST"], kind="dma")
    for kc in range(8):
        q = "sp" if kc % 2 == 0 else "act"
        A(q, "dma_start", dict(out=XT[:, kc, :], in_=xT[kc * 128:(kc + 1) * 128, :]),
          writes=[("XT", kc, t) for t in range(4)], kind="dma")
    A("sp", "dma_start", dict(out=BIAS[:], in_=bias_d), writes=["BIAS"], kind="dma")
    A("sp", "dma_start", dict(out=GVN[:], in_=gvn_d), writes=["GVN"], kind="dma")
    A("pool", "dma_start", dict(out=WCT[:], in_=wsT_d), writes=["WCT"], kind="dma")
    for h in range(8):
        A("pool", "tensor_tensor", dict(out=WCT[:, h * 128:(h + 1) * 128], in0=WCT[:, h * 128:(h + 1) * 128],
                                                 in1=MASKLE, op=ALU.mult), reads=["WCT", "CONST"], writes=["WCT"])
    A("act", "activation", dict(out=CONDF[:, 0:8], in_=vcol(V_C, 8), func=AF.Silu), reads=["VEC"], writes=["CONDF"])
    A("dve", "tensor_copy", dict(out=CONDB[:, 0:8], in_=CONDF[:, 0:8]), reads=["CONDF"], writes=["CONDB"])

    def mod_compute(l):
        for piece in range(4):
            buf = (l * 4 + piece) % 2
            A("pool", "dma_start", dict(out=MWS[buf][:], in_=modw[l, :, piece * 1152:(piece + 1) * 1152].rearrange("(kc p) n -> p kc n", p=128)),
              writes=[("MWS", buf)], kind="dma")
            for j in range(9):
                col = l * 36 + piece * 9 + j
                for kc in range(8):
                    last = (piece == 3 and j == 8 and kc == 7)
                    A("pe", "matmul", dict(out=PS[7][:, col:col + 1], lhsT=MWS[buf][:, kc, j * 128:(j + 1) * 128],
                                           rhs=CONDB[:, kc:kc + 1], start=(kc == 0), stop=(kc == 7)),
                      reads=[("MWS", buf), "CONDB"], writes=[("P", 7)] + ([("MWSDONE", l)] if last else []))
        A("dve", "tensor_copy", dict(out=MODP[:, l * 36:(l + 1) * 36], in_=PS[7][:, l * 36:(l + 1) * 36]),
          reads=[("P", 7)], writes=[("MODP", l)])
        A("sp", "dma_start", dict(out=ccmin[l], in_=MODP[:, l * 36:(l + 1) * 36]), reads=[("MODP", l)], writes=[("CCMIN", l)], kind="dma")

    def mod_gather(l):
        A("pool", "collective_compute", dict(kind="AllGather", op=ALU.bypass, replica_groups=PAIRS, ins=[ccmin[l]], outs=[ccmout[l]]),
          reads=[("CCMIN", l)], writes=[("CCMOUT", l)], kind="cc")

    def mod_finalize(l):
        A("sp", "dma_start", dict(out=MOD[:, l, :].rearrange("p (r n) -> p r n", r=2), in_=ccmout[l].rearrange("(r p) n -> p r n", p=128)),
          reads=[("CCMOUT", l)], writes=[("MOD", l)], kind="dma")
        A("dve", "tensor_tensor", dict(out=MOD[:, l, :], in0=MOD[:, l, :], in1=vcol(V_MODB + l * 72, 72), op=ALU.add),
          reads=[("MOD", l), "VEC"], writes=[("MOD", l)])
        for s in range(3):
            ls = l * 3 + s
            A("dve", "scalar_tensor_tensor", dict(
                out=AV[:, ls * 8:(ls + 1) * 8], in0=MOD[:, l, (3 * s + 1) * 8:(3 * s + 2) * 8], scalar=1.0,
                in1=vcol(V_NG + ls * 8, 8), op0=ALU.add, op1=ALU.mult), reads=[("MOD", l), "VEC"], writes=[("AV", ls)])
            A("dve", "tensor_scalar", dict(
                out=GT[:, ls * 8:(ls + 1) * 8], in0=MOD[:, l, (3 * s + 2) * 8:(3 * s + 3) * 8],
                scalar1=(1.0 if s == 1 else 0.5), scalar2=None, op0=ALU.mult), reads=[("MOD", l)], writes=[("GT", ls)])

    mod_compute(0)
    mod_compute(1)

    def Hs(kc, lo, hi):
        return HB[:, kc, lo:hi]

    def emit_norm(ls, scr32, scr16, skey):
        for t in range(4):
            emit_norm_tile(ls, t, scr32, scr16, skey)

    def emit_norm_tile(ls, t, scr32, scr16, skey):
        l = ls // 3
        s = ls % 3
        if True:
            lo, hi = t * 512, (t + 1) * 512
            for kc in range(8):
                qi = kc % 4
                sq = scr16[:, 0 if qi < 2 else 7, (qi % 2) * 512:(qi % 2) * 512 + 512]
                A("act", "activation", dict(out=sq, in_=XT[:, kc, lo:hi], func=AF.Square),
                  reads=[("XT", kc, t)], writes=[(skey, "sq", qi)])
                A("pe", "matmul", dict(out=PS[6][:, :], lhsT=ONES, rhs=sq, start=(kc == 0), stop=(kc == 7)),
                  reads=[(skey, "sq", qi), "CONST"], writes=[("P", 6)])
            A("act", "activation", dict(out=scr32[:, 1, :], in_=PS[6][:, :], func=AF.Sqrt, bias=vcol(V_EPS), scale=1.0 / D),
              reads=[("P", 6), "VEC"], writes=[(skey, 1)])
            A("dve", "reciprocal", dict(out=scr32[:, 2, :], in_=scr32[:, 1, :]), reads=[(skey, 1)], writes=[(skey, 2)])
            for kc in range(8):
                tb = 3 + kc % 2
                A("dve", "tensor_tensor", dict(out=scr32[:, tb, :], in0=XT[:, kc, lo:hi], in1=scr32[:, 2, :], op=ALU.mult),
                  reads=[("XT", kc, t), (skey, 2)], writes=[(skey, tb)])
                A("pool", "tensor_scalar", dict(
                    out=Hs(kc, lo, hi), in0=scr32[:, tb, :], scalar1=AV[:, ls * 8 + kc:ls * 8 + kc + 1],
                    scalar2=MOD[:, l, 3 * s * 8 + kc:3 * s * 8 + kc + 1], op0=ALU.mult, op1=ALU.add),
                  reads=[(skey, tb), ("AV", ls), ("MOD", l)], writes=[("H", kc, t)])

    gstate = {"g": 0, "pg": 0, "pd": 0}

    def ffn_load(l, w, gi):
        f0, nf = GROUPS[gi]
        buf = gstate["g"] % 2
        ncol = nf * 128
        for kc in range(8):
            A("pool", "dma_start", dict(out=WG[buf][:, kc, 0:ncol], in_=wg[l, w, kc * 128:(kc + 1) * 128, f0 * 128:f0 * 128 + ncol]),
              reads=([("MWSDONE", 0), ("MWSDONE", 1)] if (buf == 1 and kc == 0) else []), writes=[("WG", buf, kc)], kind="dma")
            A("pool", "dma_start", dict(out=WU[buf][:, kc, 0:ncol], in_=wu[l, w, kc * 128:(kc + 1) * 128, f0 * 128:f0 * 128 + ncol]),
              writes=[("WU", buf, kc)], kind="dma")
        for j in range(nf):
            A("pool", "dma_start", dict(out=WD[buf][:, j, :], in_=wd[l, w, (f0 + j) * 128:(f0 + j + 1) * 128, :]),
              writes=[("WD", buf, j)], kind="dma")
        gstate["g"] += 1
        return buf

    def ffn_compute(ls, gi, buf, mid_hook=None):
        f0, nf = GROUPS[gi]
        for t in range(4):
            lo, hi = t * 512, (t + 1) * 512
            for j in range(nf):
                pg = gstate["pg"] % 2
                pu = 2 + gstate["pg"] % 2
                gstate["pg"] += 1
                for kc in range(8):
                    A("pe", "matmul", dict(out=PS[pg][:, :], lhsT=WG[buf][:, kc, j * 128:(j + 1) * 128],
                                                                  rhs=Hs(kc, lo, hi), start=(kc == 0), stop=(kc == 7)),
                      reads=[("WG", buf, kc), ("H", kc, t)], writes=[("P", pg)])
                for kc in range(8):
                    A("pe", "matmul", dict(out=PS[pu][:, :], lhsT=WU[buf][:, kc, j * 128:(j + 1) * 128],
                                                                  rhs=Hs(kc, lo, hi), start=(kc == 0), stop=(kc == 7)),
                      reads=[("WU", buf, kc), ("H", kc, t)], writes=[("P", pu)])
                sg = 5 + pg
                A("act", "activation", dict(out=SQB[:, sg, :], in_=PS[pg][:, :], func=AF.Silu),
                  reads=[("P", pg)], writes=[("S", sg)])
                A("dve", "tensor_tensor", dict(out=ACTT[:, j, :], in0=SQB[:, sg, :], in1=PS[pu][:, :], op=ALU.mult),
                  reads=[("S", sg), ("P", pu)], writes=[("ACTT", j)])
            if mid_hook is not None:
                mid_hook(t)
            for oc in range(8):
                pd = 4 + gstate["pd"] % 2
                gstate["pd"] += 1
                for j in range(nf):
                    A("pe", "matmul", dict(out=PS[pd][:, :], lhsT=WD[buf][:, j, oc * 128:(oc + 1) * 128],
                                                                  rhs=ACTT[:, j, :], start=(j == 0), stop=(j == nf - 1)),
                      reads=[("WD", buf, j), ("ACTT", j)], writes=[("P", pd)])
                A("dve", "scalar_tensor_tensor", dict(
                    out=XT[:, oc, lo:hi], in0=PS[pd][:, :], scalar=GT[:, ls * 8 + oc:ls * 8 + oc + 1],
                    in1=XT[:, oc, lo:hi], op0=ALU.mult, op1=ALU.add),
                  reads=[("P", pd), ("GT", ls), ("XT", oc, t)], writes=[("XT", oc, t)])

    def emit_ffn(l, w, barrier=True, preloaded=None, prefetch=None, pre_last=None):
        ls = l * 3 + 2 * w
        if barrier:
            P.barrier()
        b0 = preloaded if preloaded is not None else ffn_load(l, w, 0)
        emit_norm_tile(ls, 0, SQB, SQB16, "S")
        bufs = [b0]
        nxt = None

        def hook(t):
            if t + 1 < 4:
                emit_norm_tile(ls, t + 1, SQB, SQB16, "S")

        for gi in range(4):
            if gi + 1 < 4:
                bufs.append(ffn_load(l, w, gi + 1))
            elif prefetch is not None:
                nxt = ffn_load(prefetch[0], prefetch[1], 0)
            elif pre_last is not None:
                pre_last()
            ffn_compute(ls, gi, bufs[gi], mid_hook=(hook if gi == 0 else None))
        return nxt

    def dump_and_finish():
        outs = []
        for kc in range(8):
            outs.append(A("sp", "dma_start", dict(out=outT[kc * 128:(kc + 1) * 128, :], in_=XT[:, kc, :]),
                          reads=[("XT", kc, t) for t in range(4)], kind="dma"))
        A("sp", None, None)
        P.ops[-1]["deps"] = outs

    BUF0_KEYS = [("WG", 0, kc) for kc in range(8)] + [("WU", 0, kc) for kc in range(8)] + [("WD", 0, j) for j in range(6)]

    def prefetch_win():
        assert gstate["g"] % 2 == 0
        for kc in range(7):
            A("pool", "dma_start", dict(out=WIN[:, kc, :], in_=hy_in[kc * 128:(kc + 1) * 128, :], max_dma_last_dim=4096),
              writes=[("WIN", kc)] + (BUF0_KEYS if kc == 0 else []), kind="dma")

    def prefetch_win1():
        assert gstate["g"] % 2 == 0
        for kc in range(6):
            A("pool", "dma_start", dict(out=WIN1[:, kc, :], in_=sc_in[kc * 128:(kc + 1) * 128, :], max_dma_last_dim=4096),
              writes=[("WIN1", kc)] + (BUF0_KEYS if kc == 0 else []), kind="dma")

    pre0 = ffn_load(0, 0, 0)
    mod_gather(0)
    mod_gather(1)
    mod_finalize(0)
    emit_ffn(0, 0, barrier=False, preloaded=pre0, pre_last=prefetch_win)
    if stop_after == "ffn00":
        P.barrier()
        dump_and_finish()
        return nc, P

    P.barrier()
    for kc in range(7, 8):
        A("pool", "dma_start", dict(out=WIN[:, kc, :], in_=hy_in[kc * 128:(kc + 1) * 128, :], max_dma_last_dim=4096),
          writes=[("WIN", kc)], kind="dma")
    mod_finalize(1)
    emit_norm(1, RS, RS16, "R")
    P.barrier()
    rot = {"b": 0}

    def nbank(n=6):
        b = rot["b"] % n
        rot["b"] += 1
        return b

    kv_dmas = []
    def kv_part(t):
        lo, hi = t * 512, (t + 1) * 512
        for c in range(4):
            pb = nbank()
            for kc in range(8):
                A("pe", "matmul", dict(out=PS[pb][:, :], lhsT=WIN[:, kc, 1536 + c * 128:1536 + (c + 1) * 128], rhs=Hs(kc, lo, hi),
                                                              start=(kc == 0), stop=(kc == 7)),
                  reads=[("WIN", kc), ("H", kc, t)], writes=[("P", pb)])
            kb = c % 2
            A("dve", "tensor_copy", dict(out=ACTT[:, kb, :], in_=PS[pb][:, :]),
              reads=[("P", pb)], writes=[("KST", kb)])
            kv_dmas.append(A("sp", "dma_start", dict(out=ccink[c * 128:(c + 1) * 128, lo:hi], in_=ACTT[:, kb, :]),
                             reads=[("KST", kb)], writes=[("CCIN", "k", c, t)], kind="dma"))
        for bl in range(4):
            n0 = lo + bl * 128
            B = n0 // 128
            pb = nbank()
            for kc in range(8):
                A("pe", "matmul", dict(out=PS[pb][:, :], lhsT=HB[:, kc, n0:n0 + 128], rhs=WIN[:, kc, 2048:2560],
                                                                 start=(kc == 0), stop=(kc == 7)),
                  reads=[("WIN", kc), ("H", kc, t)], writes=[("P", pb)])
            kb = 2 + bl % 2
            A("act", "activation", dict(out=ACTT[:, kb, :], in_=PS[pb][:, :], func=AF.Copy),
              reads=[("P", pb)], writes=[("KST", kb)])
            kv_dmas.append(A("act", "dma_start", dict(
                out=ccinv[32 * B:32 * B + 32, :].rearrange("r (q j) -> (r q) j", q=4), in_=ACTT[:, kb, :]),
                reads=[("KST", kb)], writes=[("CCIN", "v", B)], kind="dma"))

    def main_part(t):
        lo, hi = t * 512, (t + 1) * 512
        for c in range(4):
            pb = nbank()
            for kc in range(8):
                A("pe", "matmul", dict(out=PS[pb][:, :], lhsT=WIN[:, kc, c * 128:(c + 1) * 128], rhs=Hs(kc, lo, hi),
                                                              start=(kc == 0), stop=(kc == 7)),
                  reads=[("WIN", kc), ("H", kc, t)], writes=[("P", pb)])
            A("act", "activation", dict(out=RS16[:, 4 + c // 2, (c % 2) * 512:(c % 2) * 512 + 512], in_=PS[pb][:, :], func=AF.Gelu),
              reads=[("P", pb)], writes=[("UT", c)])
        VN_ = [RS16[:, 3, 0:512], RS16[:, 3, 512:1024], ACTT[:, 4, :], ACTT[:, 5, :]]
        VG_ = [RS[:, 0, :], RS[:, 1, :], RS[:, 6, :], RS[:, 7, :]]
        for bl in range(4):
            n0 = lo + bl * 128
            pb = nbank()
            for kc in range(8):
                A("pe", "matmul", dict(out=PS[pb][:, :], lhsT=HB[:, kc, n0:n0 + 128], rhs=WIN[:, kc, 512:1024],
                                       start=(kc == 0), stop=(kc == 7)),
                  reads=[("WIN", kc), ("H", kc, t)], writes=[("P", pb)])
            A("act", "activation", dict(out=VG_[bl], in_=PS[pb][:, :], func=AF.Gelu),
              reads=[("P", pb)], writes=[("VG", bl)])
            A("dve", "tensor_tensor", dict(out=RS[:, 2, :], in0=VG_[bl], in1=VG_[bl], op=ALU.mult),
              reads=[("VG", bl)], writes=[("R", 2)])
            A("dve", "tensor_reduce", dict(out=MSV[:, bl, :], in_=RS[:, 2, :].rearrange("p (h d) -> p h d", d=64), axis=AX.X, op=ALU.add),
              reads=[("R", 2)], writes=[("MSV", bl)])
        A("act", "activation", dict(out=RSV[:], in_=MSV[:], func=AF.Sqrt, bias=vcol(V_EPS), scale=1.0 / 64),
          reads=[("MSV", bl) for bl in range(4)] + ["VEC"], writes=["RSV"])
        A("dve", "reciprocal", dict(out=RV[:], in_=RSV[:]), reads=["RSV"], writes=["RV"])
        for bl in range(4):
            A("dve", "tensor_tensor", dict(out=VG_[bl], in0=VG_[bl], in1=GVN[:], op=ALU.mult),
              reads=[("VG", bl), "GVN"], writes=[("VG", bl)])
            A("dve", "tensor_tensor", dict(
                out=VN_[bl].rearrange("p (h d) -> p h d", d=64), in0=VG_[bl].rearrange("p (h d) -> p h d", d=64),
                in1=RV[:, bl, :].unsqueeze(2).broadcast_to([128, 8, 64]), op=ALU.mult),
              reads=[("VG", bl), "RV"], writes=[("VN", bl)])
        for c in range(4):
            pb = nbank()
            for kc in range(8):
                A("pe", "matmul", dict(out=PS[pb][:, :], lhsT=WIN[:, kc, 1024 + c * 128:1024 + (c + 1) * 128], rhs=Hs(kc, lo, hi),
                                                              start=(kc == 0), stop=(kc == 7)),
                  reads=[("WIN", kc), ("H", kc, t)], writes=[("P", pb)])
            A("act", "activation", dict(out=QT[:, c, lo:hi], in_=PS[pb][:, :], func=AF.Copy),
              reads=[("P", pb)], writes=[("QT", c, t)])
        for bl in range(4):
            n0 = lo + bl * 128
            for c in range(4):
                sb = 6 + (c % 2)
                for hh in range(2):
                    A("pe", "matmul", dict(
                        out=PS[sb][:, hh * 128:(hh + 1) * 128], lhsT=VN_[bl][:, c * 128:(c + 1) * 128],
                        rhs=WCT[:, (2 * c + hh) * 128:(2 * c + hh + 1) * 128], start=True, stop=True),
                      reads=[("VN", bl), "WCT"], writes=[("P", sb)])
                tb = c % 2
                for hh in range(2):
                    A("dve", "tensor_tensor", dict(
                        out=T1[hh * 64:(hh + 1) * 64, tb, :], in0=PS[sb][hh * 64:(hh + 1) * 64, hh * 128:(hh + 1) * 128],
                        in1=BIAS[hh * 64:(hh + 1) * 64, c * 128:(c + 1) * 128], op=ALU.add),
                      reads=[("P", sb), "BIAS"], writes=[("R6", tb, hh)])
                A("pool", "tensor_tensor", dict(
                    out=YA[:, c, n0:n0 + 128], in0=T1[:, tb, :],
                    in1=RS16[:, 4 + c // 2, (c % 2) * 512 + bl * 128:(c % 2) * 512 + (bl + 1) * 128], op=ALU.mult),
                  reads=[("R6", tb, 0), ("R6", tb, 1), ("UT", c)], writes=[("YA", c, t)])
    for t in range(4):
        kv_part(t)
    cck = A("pool", "collective_compute", dict(kind="AllGather", op=ALU.bypass, replica_groups=PAIRS, ins=[ccink], outs=[ccoutk]),
            reads=[], writes=["CCOUTK"], kind="cc")
    P.ops[cck]["deps"] = list(set(P.ops[cck]["deps"]) | set(kv_dmas))
    ccv = A("pool", "collective_compute", dict(kind="AllGather", op=ALU.bypass, replica_groups=PAIRS, ins=[ccinv], outs=[ccoutv]),
            reads=[], writes=["CCOUTV"], kind="cc")
    P.ops[ccv]["deps"] = list(set(P.ops[ccv]["deps"]) | set(kv_dmas))
    for t in range(4):
        main_part(t)
    if stop_after == "inproj":
        P.barrier()
        dump_and_finish()
        return nc, P
    P.barrier()
    A("sp", "dma_start", dict(out=VALL[:, 0:16, :], in_=ccoutv[0:512, :].rearrange("r (q j) -> (r q) j", q=4).rearrange("(b p) j -> p b j", p=128)),
      reads=["CCOUTV"], writes=["VO"], kind="dma")
    A("act", "dma_start", dict(out=VALL[:, 16:32, :], in_=ccinv[0:512, :].rearrange("r (q j) -> (r q) j", q=4).rearrange("(b p) j -> p b j", p=128)),
      reads=[], writes=["VOWN"], kind="dma")
    for i in range(4):
        A("dve", "tensor_scalar", dict(out=VALL[:, 4 * i:4 * i + 4, :], in0=VALL[:, 4 * i:4 * i + 4, :], scalar1=vcol(V_A), scalar2=None, op0=ALU.mult),
          reads=["VO", "VEC"], writes=["VO"])

    if stop_after == "cc":
        P.barrier()
        dump_and_finish()
        return nc, P
    def kts_load(c):
        b = c % 2
        A("sp", "dma_start", dict(out=KTS[b][:, 0:2048], in_=ccoutk[c * 128:(c + 1) * 128, :]), reads=["CCOUTK"], writes=[("KTS", b, 0)], kind="dma")
        A("act", "dma_start", dict(out=KTS[b][:, 2048:4096], in_=ccink[c * 128:(c + 1) * 128, :]), reads=[], writes=[("KTS", b, 1)], kind="dma")

    def v2(ap):
        return ap.rearrange("p (h n) -> p h n", h=2)

    E_ = [WK8[:, 0:2, :], WK8[:, 2:4, :], RS[:, 0:2, :]]
    G_ = [v2(RS16[:, 2, :]), v2(RS16[:, 3, :])]
    SP_ = [v2(RS16[:, 4, :]), v2(RS16[:, 5, :])]
    W_ = [v2(RS16[:, 6, :]), v2(RS16[:, 7, :])]
    SR_ = [ACTT[:, 0:2, :], ACTT[:, 2:4, :]]
    MASK2 = MASKT.unsqueeze(1).broadcast_to([128, 2, 128])

    units = []
    for c in range(4):
        for g in range(4):
            blocks = [("own", i) for i in range(4 * g + 3, -1, -1)] + [("oth", i) for i in range(15, -1, -1)]
            for bi, (kind, i) in enumerate(blocks):
                diag = kind == "own" and i >= 4 * g
                c0 = 128 * (i - 4 * g) if diag else 0
                units.append(dict(c=c, g=g, kind=kind, i=i, diag=diag, c0=c0, first=(bi == 0), last=(bi == len(blocks) - 1),
                                  bi=bi, pg=c * 4 + g))
    kts_load(0)

    def QK(u, n):
        c, g, c0 = u["c"], u["g"], u["c0"]
        zb = 0 if n % 2 == 0 else 6
        kcol = (2048 if u["kind"] == "own" else 0) + u["i"] * 128
        q0 = g * 512 + c0
        q1 = (g + 1) * 512
        kb = c % 2
        ksl = 1 if u["kind"] == "own" else 0
        if u["first"] and g == 0 and c + 1 < 4:
            kts_load(c + 1)
        for hh in range(2):
            A("pe", "matmul", dict(out=PS[zb + hh][:, c0:512], lhsT=KTS[kb][hh * 64:(hh + 1) * 64, kcol:kcol + 128],
                                   rhs=QT[hh * 64:(hh + 1) * 64, c, q0:q1], start=True, stop=True),
              reads=[("KTS", kb, ksl), ("QT", c, g)], writes=[("P", zb + hh)])

    def S1a(u, n):
        c0 = u["c0"]
        eb = n % 3
        zb = 0 if n % 2 == 0 else 6
        A("act", "activation", dict(out=E_[eb][:, :, c0:512], in_=P2(zb)[:, :, c0:512], func=AF.Exp, scale=0.125),
          reads=[("P", zb), ("P", zb + 1)], writes=[("E", eb)])

    def S1b(u, n):
        c0 = u["c0"]
        eb = n % 3
        sb = n % 2
        A("act", "activation", dict(out=SP_[sb][:, :, c0:512], in_=E_[eb][:, :, c0:512], func=AF.Ln, bias=vcol(V_ONE), scale=1.0),
          reads=[("E", eb), "VEC"], writes=[("SP", sb)])
        if u["diag"]:
            A("pool", "tensor_tensor", dict(out=SP_[sb][:, :, c0:c0 + 128], in0=SP_[sb][:, :, c0:c0 + 128], in1=MASK2, op=ALU.mult),
              reads=[("SP", sb), "CONST"], writes=[("SP", sb)])

    def S2a(u, n):
        c0 = u["c0"]
        sb = n % 2
        cur = u["bi"] % 2
        nxt = (u["bi"] + 1) % 2
        if u["first"]:
            for i in range(2):
                A("pool", "memset", dict(ap=SR_[i], constant=0.0), writes=[("SR", i)])
        for hh in range(2):
            A("pe", "matmul", dict(out=PS[2 + hh][:, c0:512], lhsT=TRI, rhs=SP_[sb][:, hh, c0:512], start=True, stop=u["first"]),
              reads=[("SP", sb), "CONST"], writes=[("P", 2 + hh)])
            if not u["first"]:
                A("pe", "matmul", dict(out=PS[2 + hh][:, c0:512], lhsT=ONES, rhs=SR_[cur][:, hh, c0:512], start=False, stop=True),
                  reads=[("SR", cur), "CONST"], writes=[("P", 2 + hh)])
        if not u["last"]:
            A("dve", "tensor_tensor", dict(out=SR_[nxt][:, :, c0:512], in0=SR_[cur][:, :, c0:512], in1=SP_[sb][:, :, c0:512], op=ALU.add),
              reads=[("SR", cur), ("SP", sb)], writes=[("SR", nxt)])

    def S2b(u, n):
        c0 = u["c0"]
        gb = n % 2
        A("act", "activation", dict(out=G_[gb][:, :, c0:512], in_=P2(2)[:, :, c0:512], func=AF.Exp, scale=-1.0),
          reads=[("P", 2), ("P", 3)], writes=[("G", gb)])

    def S3(u, n):
        c, g, c0 = u["c"], u["g"], u["c0"]
        eb = n % 3
        gb = n % 2
        wb = n % 2
        vblk = (16 if u["kind"] == "own" else 0) + u["i"]
        A("dve", "tensor_tensor", dict(out=W_[wb][:, :, c0:512], in0=E_[eb][:, :, c0:512], in1=G_[gb][:, :, c0:512], op=ALU.mult),
          reads=[("E", eb), ("G", gb)], writes=[("W", wb)])
        if u["diag"]:
            A("pool", "tensor_tensor", dict(out=W_[wb][:, :, c0:c0 + 128], in0=W_[wb][:, :, c0:c0 + 128], in1=MASK2, op=ALU.mult),
              reads=[("W", wb), "CONST"], writes=[("W", wb)])
        for hh in range(2):
            yb = 4 + hh
            h = 2 * c + hh
            if u["first"]:
                A("pe", "matmul", dict(out=PS[yb][0:64, :], lhsT=ZEROS[:, 0:64], rhs=VALL[:, 16, :], start=True, stop=False),
                  reads=["CONST", "VOWN"], writes=[("P", yb)])
            A("pe", "matmul", dict(out=PS[yb][0:64, c0:512], lhsT=VALL[:, vblk, h * 64:(h + 1) * 64], rhs=W_[wb][:, hh, c0:512],
                                   start=False, stop=u["last"]),
              reads=[("W", wb), "VO" if u["kind"] == "oth" else "VOWN"], writes=[("P", yb)])
            if u["last"]:
                A("act", "activation", dict(out=YB[hh * 64:(hh + 1) * 64, c, g * 512:(g + 1) * 512], in_=PS[yb][0:64, :], func=AF.Copy),
                  reads=[("P", yb)], writes=[("YB", c, g, hh)])

    NU = len(units)
    QK(units[0], 0)
    for n in range(NU + 2):
        if n + 1 < NU:
            QK(units[n + 1], n + 1)
        if n < NU:
            S1a(units[n], n)
        if 0 <= n - 2 < NU:
            S2b(units[n - 2], n - 2)
        if 0 <= n - 1 < NU:
            S2a(units[n - 1], n - 1)
        if n < NU:
            S1b(units[n], n)
        if 0 <= n - 2 < NU:
            S3(units[n - 2], n - 2)

    if stop_after == "attn":
        P.barrier()
        for kc in range(8):
            for t in range(4):
                src = YA[:, kc, t * 512:(t + 1) * 512] if kc < 4 else YB[:, kc - 4, t * 512:(t + 1) * 512]
                A("dve", "tensor_copy", dict(out=XT[:, kc, t * 512:(t + 1) * 512], in_=src), reads=[], writes=[("XT", kc, t)])
        P.barrier()
        dump_and_finish()
        return nc, P
    P.barrier()
    for kc in range(8):
        A("pool", "dma_start", dict(out=WO[:, kc, :], in_=hy_out[kc * 128:(kc + 1) * 128, :]), writes=[("WO", kc)], kind="dma")
    for t in range(4):
        lo, hi = t * 512, (t + 1) * 512
        for oc in range(8):
            pb = nbank()
            for kc in range(8):
                src = YA[:, kc, lo:hi] if kc < 4 else YB[:, kc - 4, lo:hi]
                rk = [("YA", kc, t)] if kc < 4 else [("YB", kc - 4, t, 0), ("YB", kc - 4, t, 1)]
                A("pe", "matmul", dict(out=PS[pb][:, :], lhsT=WO[:, kc, oc * 128:(oc + 1) * 128], rhs=src,
                                                                          start=(kc == 0), stop=(kc == 7)),
                  reads=[("WO", kc)] + rk, writes=[("P", pb)])
            A("dve", "scalar_tensor_tensor", dict(
                out=XT[:, oc, lo:hi], in0=PS[pb][:, :], scalar=GT[:, 8 + oc:8 + oc + 1], in1=XT[:, oc, lo:hi], op0=ALU.mult, op1=ALU.add),
              reads=[("P", pb), ("GT", 1), ("XT", oc, t)], writes=[("XT", oc, t)])
    if stop_after == "mix0":
        P.barrier()
        dump_and_finish()
        return nc, P

    pre = emit_ffn(0, 1, prefetch=(1, 0))
    emit_ffn(1, 0, barrier=False, preloaded=pre, pre_last=prefetch_win1)
    if stop_after == "ffn10":
        P.barrier()
        dump_and_finish()
        return nc, P

    P.barrier()
    for kc in range(6, 8):
        A("pool", "dma_start", dict(out=WIN1[:, kc, :], in_=sc_in[kc * 128:(kc + 1) * 128, :], max_dma_last_dim=4096), writes=[("WIN1", kc)], kind="dma")
    for kc in range(8):
        A("pool", "dma_start", dict(out=WO1[:, kc, :], in_=sc_out[kc * 128:(kc + 1) * 128, :]), writes=[("WO1", kc)], kind="dma")
    emit_norm(4, SQB, SQB16, "S")
    P.barrier()
    for c in range(8):
        for which in range(2):
            colw = 1024 * (1 + which) + c * 128
            for kc in range(8):
                A("pe", "matmul", dict(
                    out=PS[7][:, which * 16 + 2 * c:which * 16 + 2 * c + 2], lhsT=WIN1[:, kc, colw:colw + 128], rhs=HB[:, kc, NT - 2:NT],
                    start=(kc == 0), stop=(kc == 7)), reads=[("WIN1", kc), ("H", kc, 3)], writes=[("P", 7)])
    A("act", "activation", dict(out=TAIL[:, 16:32], in_=PS[7][:, 0:16], func=AF.Copy), reads=[("P", 7)], writes=["TAILc"])
    A("dve", "tensor_tensor", dict(out=TAIL[:, 0:16], in0=TAIL[:, 16:32], in1=PS[7][:, 16:32], op=ALU.mult), reads=["TAILc", ("P", 7)], writes=["TAIL"])
    A("sp", "dma_start", dict(out=cc2in, in_=TAIL[:, 0:16]), reads=["TAIL"], writes=["CC2IN"], kind="dma")
    A("pool", "collective_compute", dict(kind="AllGather", op=ALU.bypass, replica_groups=PAIRS, ins=[cc2in], outs=[cc2out]),
      reads=["CC2IN"], writes=["CC2OUT"], kind="cc")
    A("sp", "dma_start", dict(out=HALOP[:], in_=cc2out[0:128, :]), reads=["CC2OUT"], writes=["HALOP"], kind="dma")
    A("dve", "tensor_scalar", dict(out=HALOP[:], in0=HALOP[:], scalar1=vcol(V_A), scalar2=None, op0=ALU.mult), reads=["HALOP", "VEC"], writes=["HALOP"])
    CXHt = T("CXH", [128, 1028], F32, O_QB + 2048)
    for t in range(4):
        lo, hi = t * 512, (t + 1) * 512
        for c in range(8):
            pbk = [(c % 2) * 3 + i for i in range(3)]
            for which in range(3):
                colw = 1024 * which + c * 128
                for kc in range(8):
                    A("pe", "matmul", dict(
                        out=PS[pbk[which]][:, :], lhsT=WIN1[:, kc, colw:colw + 128], rhs=Hs(kc, lo, hi), start=(kc == 0), stop=(kc == 7)),
                      reads=[("WIN1", kc), ("H", kc, t)], writes=[("P", pbk[which])])
            cxo = (c % 2) * 514
            A("act", "activation", dict(out=SQB[:, 0, :], in_=PS[pbk[1]][:, :], func=AF.Copy), reads=[("P", pbk[1])], writes=[("S", 0)])
            A("dve", "tensor_tensor", dict(out=CXHt[:, cxo + 2:cxo + 514], in0=SQB[:, 0, :], in1=PS[pbk[2]][:, :], op=ALU.mult),
              reads=[("S", 0), ("P", pbk[2])], writes=[("CXH", c % 2, 1)])
            if t == 0:
                A("pool", "tensor_copy", dict(out=CXHt[:, cxo:cxo + 2], in_=HALOP[:, 2 * c:2 * c + 2]), reads=["HALOP"], writes=[("CXH", c % 2, 0)])
            else:
                A("pool", "tensor_copy", dict(out=CXHt[:, cxo:cxo + 2], in_=PREV[:, 2 * c:2 * c + 2]), reads=[("PREV", c)], writes=[("CXH", c % 2, 0)])
            A("pool", "tensor_copy", dict(out=PREV[:, 2 * c:2 * c + 2], in_=CXHt[:, cxo + 512:cxo + 514]), reads=[("CXH", c % 2, 1)], writes=[("PREV", c)])
            yk = [("CXH", c % 2, 0), ("CXH", c % 2, 1), "VEC"]
            A("dve", "tensor_scalar", dict(out=SQB[:, 5, :], in0=CXHt[:, cxo + 2:cxo + 514], scalar1=vcol(V_CW + 16 + c), scalar2=None, op0=ALU.mult),
              reads=yk, writes=[("S", 5)])
            A("dve", "scalar_tensor_tensor", dict(out=SQB[:, 5, :], in0=CXHt[:, cxo + 1:cxo + 513], scalar=vcol(V_CW + 8 + c), in1=SQB[:, 5, :], op0=ALU.mult, op1=ALU.add),
              reads=yk + [("S", 5)], writes=[("S", 5)])
            A("dve", "scalar_tensor_tensor", dict(out=SQB[:, 5, :], in0=CXHt[:, cxo:cxo + 512], scalar=vcol(V_CW + c), in1=SQB[:, 5, :], op0=ALU.mult, op1=ALU.add),
              reads=yk + [("S", 5)], writes=[("S", 5)])
            A("dve", "tensor_tensor", dict(out=YC[:, c, :], in0=SQB[:, 5, :], in1=PS[pbk[0]][:, :], op=ALU.mult),
              reads=[("S", 5), ("P", pbk[0])], writes=[("YC", c)])
        for oc in range(8):
            pb = 6 + oc % 2
            for kc in range(8):
                A("pe", "matmul", dict(out=PS[pb][:, :], lhsT=WO1[:, kc, oc * 128:(oc + 1) * 128], rhs=YC[:, kc, :],
                                                                start=(kc == 0), stop=(kc == 7)),
                  reads=[("WO1", kc), ("YC", kc)], writes=[("P", pb)])
            A("dve", "scalar_tensor_tensor", dict(
                out=XT[:, oc, lo:hi], in0=PS[pb][:, :], scalar=GT[:, 32 + oc:32 + oc + 1], in1=XT[:, oc, lo:hi], op0=ALU.mult, op1=ALU.add),
              reads=[("P", pb), ("GT", 4), ("XT", oc, t)], writes=[("XT", oc, t)])
    if stop_after == "mix1":
        P.barrier()
        dump_and_finish()
        return nc, P

    emit_ffn(1, 1)

    P.barrier()
    outs = []
    for t in range(4):
        lo, hi = t * 512, (t + 1) * 512
        for kc in range(8):
            sq = SQB16[:, 0, (kc % 2) * 512:(kc % 2) * 512 + 512]
            A("act", "activation", dict(out=sq, in_=XT[:, kc, lo:hi], func=AF.Square), reads=[("XT", kc, t)], writes=[("S", 0, kc % 2)])
            A("pe", "matmul", dict(out=PS[6][:, :], lhsT=ONES, rhs=sq, start=(kc == 0), stop=(kc == 7)),
              reads=[("S", 0, kc % 2), "CONST"], writes=[("P", 6)])
        A("act", "activation", dict(out=SQB[:, 1, :], in_=PS[6][:, :], func=AF.Sqrt, bias=vcol(V_EPS), scale=1.0 / D), reads=[("P", 6), "VEC"], writes=[("S", 1)])
        A("dve", "reciprocal", dict(out=SQB[:, 2, :], in_=SQB[:, 1, :]), reads=[("S", 1)], writes=[("S", 2)])
        for kc in range(8):
            ob = 3 + kc % 4
            A("dve", "scalar_tensor_tensor", dict(out=SQB[:, ob, :], in0=XT[:, kc, lo:hi], scalar=vcol(V_FG + kc), in1=SQB[:, 2, :], op0=ALU.mult, op1=ALU.mult),
              reads=[("XT", kc, t), ("S", 2), "VEC"], writes=[("S", ob)])
            q = "sp" if kc % 2 == 0 else "act"
            outs.append(A(q, "dma_start", dict(out=outT[kc * 128:(kc + 1) * 128, lo:hi], in_=SQB[:, ob, :]), reads=[("S", ob)], kind="dma"))
    A("sp", None, None)
    P.ops[-1]["deps"] = outs
    return nc, P


_CACHE = {}


def get_nc(stop_after=None):
    if stop_after in _CACHE:
        return _CACHE[stop_after]
    nc, P = build_nc(stop_after)
    sems = {k: nc.alloc_semaphore(name="s_" + k) for k in ("act", "dve", "pool", "pe")}
    sems["async"] = [nc.alloc_semaphore(name=f"s_as{i}") for i in range(48)]
    P.prepare(48)
    with nc.Block() as block:
        @block.sync
        def _(e):
            P.emit_engine(sems, "sp", e)

        @block.scalar
        def _(e):
            P.emit_engine(sems, "act", e)

        @block.vector
        def _(e):
            P.emit_engine(sems, "dve", e)

        @block.gpsimd
        def _(e):
            P.emit_engine(sems, "pool", e)

        @block.tensor
        def _(e):
            P.emit_engine(sems, "pe", e)
    _CACHE[stop_after] = nc
    return nc


def make_inputs(x, c, mod_w, mod_b, norm_g, ffn_w_gate, ffn_w_up, ffn_w_down, hy_w_in, hy_w_out, gm_vnorm_g, gm_w_s, gm_b_s,
                sc_w_in, sc_conv_w, sc_w_out, final_norm_g):
    f = np.float32
    x = np.asarray(x, f); c = np.asarray(c, f)
    idx = np.arange(128)
    constb = np.zeros((128, 768), f)
    constb[:, 0:128] = 1.0
    constb[:, 128:256] = (idx[:, None] >= idx[None, :])
    constb[:, 256:384] = (idx[:, None] < idx[None, :])
    constb[:, 384:512] = (idx[:, None] <= idx[None, :])
    gvn = np.ascontiguousarray(np.broadcast_to(np.asarray(gm_vnorm_g, f)[0][None, :], (128, 512)))
    bs = np.asarray(gm_b_s, f)[0]
    gbias = np.zeros((128, 4, 128), f)
    for cc in range(4):
        gbias[0:64, cc, :] = bs[2 * cc][None, :]
        gbias[64:128, cc, :] = bs[2 * cc + 1][None, :]
    gbias = gbias.reshape(128, 512)
    ws = np.asarray(gm_w_s, f)[0]
    wsT = np.ascontiguousarray(ws.transpose(2, 0, 1)).reshape(128, 1024)
    base = dict(
        constb=constb, gvn=gvn, gbias=gbias, wsT=wsT,
        wg=np.ascontiguousarray(ffn_w_gate, f), wu=np.ascontiguousarray(ffn_w_up, f),
        wd=np.ascontiguousarray(ffn_w_down, f), hy_in=np.ascontiguousarray(np.asarray(hy_w_in, f)[0]),
        hy_out=np.ascontiguousarray(np.asarray(hy_w_out, f)[0]), sc_in=np.ascontiguousarray(np.asarray(sc_w_in, f)[0]),
        sc_out=np.ascontiguousarray(np.asarray(sc_w_out, f)[0]))
    mod_b = np.asarray(mod_b, f); norm_g = np.asarray(norm_g, f); mod_w = np.asarray(mod_w, f)
    cw = np.asarray(sc_conv_w, f)[0]
    fg = np.asarray(final_norm_g, f)
    in_maps = []
    for core in range(8):
        b, r = core // 2, core % 2
        vec = np.zeros((128, NV), f)
        for l in range(2):
            vec[:, V_MODB + l * 72:V_MODB + (l + 1) * 72] = mod_b[l].reshape(72, 128).T
            for s in range(3):
                ls = l * 3 + s
                vec[:, V_NG + ls * 8:V_NG + (ls + 1) * 8] = norm_g[l, s].reshape(8, 128).T
        vec[:, V_FG:V_FG + 8] = fg.reshape(8, 128).T
        for k in range(3):
            vec[:, V_CW + k * 8:V_CW + (k + 1) * 8] = cw[k].reshape(8, 128).T
        vec[:, V_A] = float(r)
        vec[:, V_EPS] = EPS
        vec[:, V_ONE] = 1.0
        vec[:, V_C:V_C + 8] = c[b].reshape(8, 128).T
        m = dict(base)
        m["xT"] = np.ascontiguousarray(x[b, r * NT:(r + 1) * NT, :].T)
        m["vec"] = vec
        m["modw"] = np.ascontiguousarray(mod_w[:, :, r * 4608:(r + 1) * 4608])
        in_maps.append(m)
    return in_maps


def kernel(**inputs):
    nc = get_nc(STOP_AFTER)
    in_maps = make_inputs(**inputs)
    res = run_bass_kernel_spmd(nc, in_maps, core_ids=list(range(8)))
    out = np.empty((4, 2 * NT, D), np.float32)
    for core in range(8):
        b, r = core // 2, core % 2
        out[b, r * NT:(r + 1) * NT, :] = res.results[core]["outT"].T
    return out
```
